# Optimizing a Trainium2 kernel written in Bass

```python
import math
import jax, jax.numpy as jnp
from jax import lax
import numpy as np

D_MODEL = 1024
BATCH = 8
SEQ = 4096
DEPTH = 4

N_A_LAYERS = DEPTH // 2
N_B_LAYERS = DEPTH - N_A_LAYERS
A_HEADS = 8
A_HEAD_DIM = D_MODEL // A_HEADS // 2
B_HEADS = 16
B_HEAD_DIM = D_MODEL // B_HEADS
D_FF = 2816
CONV_WIDTH = 3
N_BUCKETS = 32
MAX_DISTANCE = 128
Q_BLOCK = 128
RMS_EPS = 1e-6
SUBLN_EPS = 1e-5
NEG_INF = -1e30

kernel_name = "yoco_diffattn_fox_convffn"


def rms_norm(x, gain, eps=RMS_EPS):
    xf = x.astype(jnp.float32)
    y = xf * lax.rsqrt(jnp.mean(xf * xf, axis=-1, keepdims=True) + eps)
    return (y * gain.astype(jnp.float32)).astype(x.dtype)


def t5_bucket(dist):
    max_exact = N_BUCKETS // 2
    d = jnp.maximum(dist, 0)
    log_ratio = jnp.log(jnp.maximum(d, 1).astype(jnp.float32) / max_exact) / math.log(MAX_DISTANCE / max_exact)
    large = jnp.minimum(max_exact + (log_ratio * (N_BUCKETS - max_exact)).astype(jnp.int32), N_BUCKETS - 1)
    return jnp.where(d < max_exact, d, large)


def lambda_init_fn(layer_idx):
    return 0.8 - 0.6 * math.exp(-0.3 * layer_idx)


def diff_attention(x, w_qkv, w_o, lam_q1, lam_k1, lam_q2, lam_k2, subln_g, rel_bias, lambda_init):
    B, S, _ = x.shape
    qkv = x @ w_qkv
    nq = 2 * A_HEADS * A_HEAD_DIM
    q, k, v = jnp.split(qkv, [nq, 2 * nq], axis=-1)
    q = q.reshape(B, S, A_HEADS, 2, A_HEAD_DIM).astype(jnp.float32) * (A_HEAD_DIM ** -0.5)
    k = k.reshape(B, S, A_HEADS, 2, A_HEAD_DIM).astype(jnp.float32)
    v = v.reshape(B, S, A_HEADS, 2 * A_HEAD_DIM).astype(jnp.float32)
    lam = (jnp.exp(jnp.sum(lam_q1.astype(jnp.float32) * lam_k1.astype(jnp.float32)))
           - jnp.exp(jnp.sum(lam_q2.astype(jnp.float32) * lam_k2.astype(jnp.float32)))
           + lambda_init)
    outs = []
    for i in range(S // Q_BLOCK):
        q0 = i * Q_BLOCK
        k_end = q0 + Q_BLOCK
        dist = jnp.arange(q0, k_end)[:, None] - jnp.arange(k_end)[None, :]
        bias = jnp.moveaxis(rel_bias[t5_bucket(dist)].astype(jnp.float32), -1, 0)
        logits = jnp.einsum('bqhcd,bkhcd->bchqk', q[:, q0:k_end], k[:, :k_end]) + bias[None, None]
        logits = jnp.where(dist >= 0, logits, NEG_INF)
        p = jax.nn.softmax(logits, axis=-1)
        attn = p[:, 0] - lam * p[:, 1]
        outs.append(jnp.einsum('bhqk,bkhe->bqhe', attn, v[:, :k_end]))
    o = jnp.concatenate(outs, axis=1)
    o = rms_norm(o, subln_g, SUBLN_EPS) * (1.0 - lambda_init)
    return o.reshape(B, S, A_HEADS * 2 * A_HEAD_DIM).astype(x.dtype) @ w_o


def shared_kv(h, kv_norm, w_kvf, b_f):
    B, S, _ = h.shape
    kvf = rms_norm(h, kv_norm) @ w_kvf
    k, v, f_logit = jnp.split(kvf, [B_HEADS * B_HEAD_DIM, 2 * B_HEADS * B_HEAD_DIM], axis=-1)
    k = k.reshape(B, S, B_HEADS, B_HEAD_DIM).astype(jnp.float32)
    v = v.reshape(B, S, B_HEADS, B_HEAD_DIM).astype(jnp.float32)
    log_f = jax.nn.log_sigmoid(f_logit.astype(jnp.float32) + b_f.astype(jnp.float32))
    c = jnp.cumsum(log_f, axis=1)
    return k, v, jnp.transpose(c, (0, 2, 1))


def forgetting_attention(x, w_q, w_o, k, v, c):
    B, S, _ = x.shape
    q = (x @ w_q).reshape(B, S, B_HEADS, B_HEAD_DIM).astype(jnp.float32) * (B_HEAD_DIM ** -0.5)
    outs = []
    for i in range(S // Q_BLOCK):
        q0 = i * Q_BLOCK
        k_end = q0 + Q_BLOCK
        dist = jnp.arange(q0, k_end)[:, None] - jnp.arange(k_end)[None, :]
        decay = c[:, :, q0:k_end, None] - c[:, :, None, :k_end]
        logits = jnp.einsum('bqhd,bkhd->bhqk', q[:, q0:k_end], k[:, :k_end]) + decay
        logits = jnp.where(dist >= 0, logits, NEG_INF)
        p = jax.nn.softmax(logits, axis=-1)
        outs.append(jnp.einsum('bhqk,bkhd->bqhd', p, v[:, :k_end]))
    o = jnp.concatenate(outs, axis=1)
    return o.reshape(B, S, B_HEADS * B_HEAD_DIM).astype(x.dtype) @ w_o


def conv_ffn(x, w_in, conv_w, conv_b, w_out):
    u = x @ w_in
    u = lax.conv_general_dilated(u, conv_w[:, None, :].astype(u.dtype), window_strides=(1,),
                                 padding=[(CONV_WIDTH - 1, 0)],
                                 dimension_numbers=('NWC', 'WIO', 'NWC'),
                                 feature_group_count=2 * D_FF) + conv_b
    gate, val = jnp.split(u, 2, axis=-1)
    return (jax.nn.gelu(gate) * val) @ w_out


def setup_inputs(seed: int = 0) -> dict:
    key = jax.random.key(seed)
    ks = iter(jax.random.split(key, 32))
    nrm = lambda shape, s: jax.random.normal(next(ks), shape, jnp.float32) * s
    gain = lambda shape: 1.0 + nrm(shape, 0.05)
    D, F = D_MODEL, D_FF
    return {
        "x": nrm((BATCH, SEQ, D), 1.0),
        "rel_bias": nrm((N_BUCKETS, A_HEADS), 0.2),
        "a_norm_pre": gain((N_A_LAYERS, D)),
        "a_norm_post": gain((N_A_LAYERS, D)),
        "a_w_qkv": nrm((N_A_LAYERS, D, 3 * 2 * A_HEADS * A_HEAD_DIM), D ** -0.5),
        "a_lam_q1": nrm((N_A_LAYERS, A_HEAD_DIM), 0.1),
        "a_lam_k1": nrm((N_A_LAYERS, A_HEAD_DIM), 0.1),
        "a_lam_q2": nrm((N_A_LAYERS, A_HEAD_DIM), 0.1),
        "a_lam_k2": nrm((N_A_LAYERS, A_HEAD_DIM), 0.1),
        "a_subln": gain((N_A_LAYERS, 2 * A_HEAD_DIM)),
        "a_w_o": nrm((N_A_LAYERS, 2 * A_HEADS * A_HEAD_DIM, D), D ** -0.5),
        "kv_norm": gain((D,)),
        "w_kvf": jnp.concatenate([nrm((D, 2 * B_HEADS * B_HEAD_DIM), D ** -0.5),
                                  nrm((D, B_HEADS), 0.1 * D ** -0.5)], axis=-1),
        "b_f": 3.0 + nrm((B_HEADS,), 0.1),
        "b_norm_pre": gain((N_B_LAYERS, D)),
        "b_norm_post": gain((N_B_LAYERS, D)),
        "b_w_q": nrm((N_B_LAYERS, D, B_HEADS * B_HEAD_DIM), D ** -0.5),
        "b_w_o": nrm((N_B_LAYERS, B_HEADS * B_HEAD_DIM, D), D ** -0.5),
        "ffn_norm_pre": gain((DEPTH, D)),
        "ffn_norm_post": gain((DEPTH, D)),
        "ffn_w_in": nrm((DEPTH, D, 2 * F), D ** -0.5),
        "ffn_conv_w": nrm((DEPTH, CONV_WIDTH, 2 * F), CONV_WIDTH ** -0.5),
        "ffn_conv_b": nrm((DEPTH, 2 * F), 0.02),
        "ffn_w_out": nrm((DEPTH, F, D), F ** -0.5),
    }


def reference(x, rel_bias, a_norm_pre, a_norm_post, a_w_qkv, a_lam_q1, a_lam_k1, a_lam_q2, a_lam_k2,
              a_subln, a_w_o, kv_norm, w_kvf, b_f, b_norm_pre, b_norm_post, b_w_q, b_w_o,
              ffn_norm_pre, ffn_norm_post, ffn_w_in, ffn_conv_w, ffn_conv_b, ffn_w_out):
    h = x
    for l in range(DEPTH):
        if l < N_A_LAYERS:
            a = diff_attention(rms_norm(h, a_norm_pre[l]), a_w_qkv[l], a_w_o[l],
                               a_lam_q1[l], a_lam_k1[l], a_lam_q2[l], a_lam_k2[l],
                               a_subln[l], rel_bias, lambda_init_fn(l))
            h = h + rms_norm(a, a_norm_post[l])
        else:
            if l == N_A_LAYERS:
                k_sh, v_sh, c_sh = shared_kv(h, kv_norm, w_kvf, b_f)
            j = l - N_A_LAYERS
            a = forgetting_attention(rms_norm(h, b_norm_pre[j]), b_w_q[j], b_w_o[j], k_sh, v_sh, c_sh)
            h = h + rms_norm(a, b_norm_post[j])
        f = conv_ffn(rms_norm(h, ffn_norm_pre[l]), ffn_w_in[l], ffn_conv_w[l], ffn_conv_b[l], ffn_w_out[l])
        h = h + rms_norm(f, ffn_norm_post[l])
    return h
```

```python
import math
from contextlib import ExitStack

import numpy as np
import concourse.bass as bass
import concourse.mybir as mybir
from concourse.bass_utils import run_bass_kernel_spmd

F32 = mybir.dt.float32
BF16 = mybir.dt.bfloat16
U8 = mybir.dt.uint8
AF = mybir.ActivationFunctionType
ALU = mybir.AluOpType
AX = mybir.AxisListType

S = 4096
D = 1024
DEPTH = 4
NA = 2
F = 2816
NFC = F // 128
RMS_EPS = 1e-6
SUBLN_EPS = 1e-5
N_BUCKETS = 32
MAX_DISTANCE = 128
ARENA_BYTES = 207 * 1024 + 512
DMA_RING = 8

G_A_PRE, G_A_POST, G_KV, G_B_PRE, G_B_POST, G_F_PRE, G_F_POST = 0, 2, 4, 5, 7, 9, 13
NG = 17


def lambda_init_fn(layer_idx):
    return 0.8 - 0.6 * math.exp(-0.3 * layer_idx)


class Buf:
    __slots__ = ("w", "r")

    def __init__(self):
        self.w = None
        self.r = {}


class Ctx:
    def __init__(self, nc, stack):
        self.nc = nc
        self.E = {"pe": nc.tensor, "act": nc.scalar, "dve": nc.vector, "pool": nc.gpsimd, "sp": nc.sync}
        self.semh = {}
        self.cnt = {}
        self.waited = {k: {} for k in self.E}
        for k in self.E:
            self.semh[k] = stack.enter_context(nc.semaphore("s_" + k))
            self.cnt[k] = 0
        self.dval = {}
        self.ringpos = {"sp": 0, "pool": 0}
        for q in ("sp", "pool"):
            for i in range(DMA_RING):
                key = "%sd%d" % (q, i)
                self.semh[key] = stack.enter_context(nc.semaphore("s_" + key))
                self.dval[key] = 0
        self.arena = stack.enter_context(nc.sbuf_tensor("arena", [128, ARENA_BYTES], U8))
        self.aoff = 0

    def mark(self):
        return self.aoff

    def release(self, m):
        self.aoff = m

    def tile(self, free_shape, dt):
        n = 1
        for s_ in free_shape:
            n *= s_
        sz = n * (4 if dt == F32 else 2)
        off = (self.aoff + 63) // 64 * 64
        assert off + sz <= ARENA_BYTES, "arena overflow %d" % (off + sz)
        self.aoff = off + sz
        v = self.arena[:, off:off + sz].bitcast(dt)
        if len(free_shape) == 2:
            v = v.rearrange("p (a b) -> p a b", a=free_shape[0])
        elif len(free_shape) == 3:
            v = v.rearrange("p (a b c) -> p a b c", a=free_shape[0], b=free_shape[1])
        return v

    def wait(self, eng, toks):
        wd = self.waited[eng]
        for (k, v) in toks:
            if k == eng:
                if eng == "pe":
                    continue
                if v < self.cnt[eng] - 2:
                    continue
            if v > wd.get(k, 0):
                self.E[eng].wait_ge(self.semh[k], v)
                wd[k] = v

    @staticmethod
    def _deps(reads, writes, extra):
        deps = list(extra)
        for b in reads:
            if b.w is not None:
                deps.append(b.w)
        for b in writes:
            if b.w is not None:
                deps.append(b.w)
            deps.extend(b.r.items())
        return deps

    @staticmethod
    def _mark(tok, reads, writes):
        for b in reads:
            if b.r.get(tok[0], 0) < tok[1]:
                b.r[tok[0]] = tok[1]
        for b in writes:
            b.w = tok
            b.r = {}

    def op(self, eng, fn, reads=(), writes=(), extra=()):
        self.wait(eng, self._deps(reads, writes, extra))
        ins = fn(self.E[eng])
        self.cnt[eng] += 1
        ins.then_inc(self.semh[eng], 1)
        tok = (eng, self.cnt[eng])
        self._mark(tok, reads, writes)
        return tok

    def dma(self, q, out, in_, reads=(), writes=(), extra=()):
        idx = self.ringpos[q] % DMA_RING
        self.ringpos[q] += 1
        key = "%sd%d" % (q, idx)
        deps = self._deps(reads, writes, extra)
        if self.dval[key] > 0:
            deps.append((key, self.dval[key]))
        self.wait(q, deps)
        ins = self.E[q].dma_start(out=out, in_=in_)
        self.dval[key] += 16
        ins.then_inc(self.semh[key], 16)
        tok = (key, self.dval[key])
        self._mark(tok, reads, writes)
        return tok

    def barrier(self):
        toks = [(k, self.cnt[k]) for k in self.E if self.cnt[k] > 0]
        toks += [(k, v) for k, v in self.dval.items() if v > 0]
        for eng in self.E:
            wd = self.waited[eng]
            for (k, v) in toks:
                if k == eng:
                    continue
                if v > wd.get(k, 0):
                    self.E[eng].wait_ge(self.semh[k], v)
                    wd[k] = v


class PsumRing:
    def __init__(self, aps):
        self.aps = aps
        self.bufs = [Buf() for _ in aps]
        self.i = 0

    def next(self):
        k = self.i % len(self.aps)
        self.i += 1
        return self.aps[k], self.bufs[k]


def _t5_bucket_np(d):
    max_exact = N_BUCKETS // 2
    d = np.maximum(d, 0)
    log_ratio = np.log(np.maximum(d, 1).astype(np.float32) / np.float32(max_exact)) / np.float32(
        math.log(MAX_DISTANCE / max_exact))
    large = np.minimum(max_exact + (log_ratio * (N_BUCKETS - max_exact)).astype(np.int32), N_BUCKETS - 1)
    return np.where(d < max_exact, d, large)


C_IDENT, C_J, C_TRI, C_OH, C_MK, C_NCOL = 0, 128, 256, 384, 768, 1152


def _make_consts():
    c = np.zeros((128, C_NCOL), np.float32)
    c[:, C_IDENT:C_IDENT + 128] = np.eye(128, dtype=np.float32)
    c[:, C_J:C_J + 128] = np.eye(128, dtype=np.float32)[::-1]
    k = np.arange(128)[:, None]
    q = np.arange(128)[None, :]
    c[:, C_TRI:C_TRI + 128] = (q >= k).astype(np.float32)
    i = np.arange(383)
    dd = i - 127
    bk = _t5_bucket_np(dd)
    for b in range(32):
        c[b, C_OH:C_OH + 383] = ((dd >= 0) & (bk == b)).astype(np.float32)
    c[0:8, C_MK:C_MK + 383] = (dd >= 0).astype(np.float32)[None, :]
    return c


class Prog:
    def __init__(self, layers=(0, 1, 2, 3), first_from_x=True, last_to_y=True, debug=False, stop=None):
        self.layers = tuple(layers)
        self.debug = debug
        nc = bass.Bass("TRN2", target_bir_lowering=False)
        self.nc = nc
        self.stack = ExitStack()
        ctx = Ctx(nc, self.stack)
        self.ctx = ctx

        def din(name, shape):
            return nc.dram_tensor(name, list(shape), F32, kind="ExternalInput").ap()

        self.xT = din("xT", [D, S])
        self.rel_bias = din("rel_bias", [32, 8])
        self.a_w_qkv = din("a_w_qkv", [2, D, 3 * D])
        self.a_w_o = din("a_w_o", [2, D, D])
        self.w_kvf = din("w_kvf", [D, 2064])
        self.b_w_q = din("b_w_q", [2, D, D])
        self.b_w_o = din("b_w_o", [2, D, D])
        self.ffn_w_in = din("ffn_w_in", [4, D, 2 * F])
        self.ffn_w_out = din("ffn_w_out", [4, F, D])
        self.gains = din("gains", [128, NG * 8])
        self.convp = din("convp", [128, 4 * 4 * 44])
        self.lamv = din("lamv", [1, 2 * 4 * 64])
        self.subln = din("subln", [1, 2 * 128])
        self.b_f = din("b_f", [16, 1])
        self.cst = din("cst", [128, C_NCOL])
        self.yT = nc.dram_tensor("yT", [D, S], F32, kind="ExternalOutput").ap()

        kind = "ExternalOutput" if debug else "Internal"

        def dscr(name, shape, dt):
            return nc.dram_tensor(name, list(shape), dt, kind=kind).ap()

        self.hbuf = dscr("hbuf", [D, S], F32)
        self.qkT = dscr("qkT", [16, 128, S], BF16)
        self.kTs = dscr("kTs", [8, 128, S], BF16)
        self.vA = dscr("vA", [8, 128, 32, 128], BF16)
        self.vB = dscr("vB", [8, 128, 32, 128], BF16)
        self.oT = dscr("oT", [D, S], BF16)
        self.escr = dscr("escr", [8, 384], F32)
        self.ebig = dscr("ebig", [128, 8 * 2 * 128], F32)

        self.setup()
        for l in self.layers:
            src = self.xT if l == self.layers[0] else self.hbuf
            if l < NA:
                self.phase_proj("A", l, src)
                if stop == "proj":
                    break
                self.phase_att("A", l)
                if stop == "att":
                    break
                self.phase_po(l, src, self.a_w_o[l], G_A_POST + l)
                if stop == "po":
                    break
            else:
                j = l - NA
                if l == NA:
                    self.phase_proj("KVF", l, src)
                self.phase_proj("B", l, src)
                self.phase_att("B", l)
                self.phase_po(l, src, self.b_w_o[j], G_B_POST + j)
            dst = self.yT if l == DEPTH - 1 else self.hbuf
            self.phase_ffn(l, dst)
        ctx.barrier()
        self.stack.close()

    def uname(self, base):
        self._uid = getattr(self, "_uid", 0) + 1
        return "%s_%d" % (base, self._uid)

    def setup(self):
        ctx, nc = self.ctx, self.nc
        self.identf = ctx.tile([128], F32)
        self.tri = ctx.tile([128], F32)
        self.ident_bf = ctx.tile([128], BF16)
        self.ones_mean = ctx.tile([128], F32)
        self.ones_f = ctx.tile([128], F32)
        self.gain_sb = ctx.tile([NG, 8], F32)
        self.neglam = ctx.tile([2], F32)
        self.gsub = ctx.tile([2, 128], F32)
        self.negbf = ctx.tile([1], F32)
        self.cnegT = ctx.tile([32, 16], F32)
        self.cref = ctx.tile([16, 8], F32)
        self.b_const = Buf()
        m0 = ctx.mark()
        cst = ctx.tile([C_NCOL], F32)
        cb = Buf()
        ctx.dma("sp", cst, self.cst, writes=[cb])
        gb = Buf()
        ctx.dma("sp", self.gain_sb.rearrange("p a b -> p (a b)"), self.gains, writes=[gb])
        ctx.dma("sp", self.negbf[0:16, :], self.b_f, writes=[gb])
        lam_sb = ctx.tile([8, 64], F32)
        lb = Buf()
        ctx.dma("sp", lam_sb.rearrange("p a b -> p (a b)"), bass.AP(self.lamv.tensor, 0, [[0, 128], [1, 512]]), writes=[lb])
        sub_sb = ctx.tile([2, 128], F32)
        ctx.dma("sp", sub_sb.rearrange("p a b -> p (a b)"), bass.AP(self.subln.tensor, 0, [[0, 128], [1, 256]]), writes=[lb])
        rb_sb = ctx.tile([8], F32)
        rb31 = ctx.tile([1], F32)
        ctx.dma("sp", rb_sb[0:32, :], self.rel_bias, writes=[lb])
        ctx.dma("sp", rb31[0:8, :], bass.AP(self.rel_bias.tensor, 31 * 8, [[1, 8], [1, 1]]), writes=[lb])
        k = self.b_const
        ctx.op("dve", lambda e: e.tensor_copy(self.identf, cst[:, C_IDENT:C_IDENT + 128]), reads=[cb], writes=[k])
        ctx.op("dve", lambda e: e.tensor_copy(self.tri, cst[:, C_TRI:C_TRI + 128]), reads=[cb], writes=[k])
        ctx.op("dve", lambda e: e.tensor_copy(self.ident_bf, cst[:, C_IDENT:C_IDENT + 128]), reads=[cb], writes=[k])
        ctx.op("dve", lambda e: e.memset(self.ones_mean, 1.0 / D), writes=[k])
        ctx.op("dve", lambda e: e.memset(self.ones_f, 1.0), writes=[k])
        ctx.op("dve", lambda e: e.tensor_scalar(self.negbf[0:16, :], self.negbf[0:16, :], -1.0, None, op0=ALU.mult), reads=[gb], writes=[k])
        prod = ctx.tile([4, 64], F32)
        ssum = ctx.tile([4], F32)
        tb = Buf()
        for l in range(2):
            ctx.op("dve", lambda e: e.tensor_tensor(out=prod[:, 0:2, :], in0=lam_sb[:, 4 * l:4 * l + 4:2, :],
                                                    in1=lam_sb[:, 4 * l + 1:4 * l + 4:2, :], op=ALU.mult), reads=[lb], writes=[tb])
            ctx.op("dve", lambda e: e.reduce_sum(out=ssum[:, 0:2], in_=prod[:, 0:2, :], axis=AX.X), reads=[tb], writes=[tb])
            ctx.op("act", lambda e: e.activation(out=ssum[:, 2:4], in_=ssum[:, 0:2], func=AF.Exp), reads=[tb], writes=[tb])
            ctx.op("dve", lambda e: e.tensor_tensor(out=ssum[:, 0:1], in0=ssum[:, 3:4], in1=ssum[:, 2:3], op=ALU.subtract), reads=[tb], writes=[tb])
            ctx.op("dve", lambda e: e.tensor_scalar(self.neglam[:, l:l + 1], ssum[:, 0:1], -lambda_init_fn(l), None, op0=ALU.add), reads=[tb], writes=[k])
            ctx.op("dve", lambda e: e.tensor_scalar(self.gsub[:, l, :], sub_sb[:, l, :], 1.0 - lambda_init_fn(l), None, op0=ALU.mult), reads=[lb], writes=[k])
        with nc.psum_tensor("ps_setup", [128, 512], F32) as ps:
            pb = Buf()
            ev = ctx.tile([384], F32)
            eb = Buf()
            ctx.op("dve", lambda e: e.tensor_scalar(rb31[0:8, :], rb31[0:8, :], -1.0, None, op0=ALU.mult), reads=[lb], writes=[lb])
            ctx.op("pe", lambda e: e.matmul(ps[0:8, 0:384], rb_sb[0:32, 0:8], cst[0:32, C_OH:C_OH + 384], start=True, stop=True),
                   reads=[lb, cb], writes=[pb])
            ctx.op("act", lambda e: e.activation(out=ev[0:8, :], in_=ps[0:8, 0:384], func=AF.Exp, bias=rb31[0:8, 0:1], scale=1.0),
                   reads=[pb, lb], writes=[eb])
            ctx.op("dve", lambda e: e.tensor_tensor(out=ev[0:8, :], in0=ev[0:8, :], in1=cst[0:8, C_MK:C_MK + 384], op=ALU.mult),
                   reads=[eb, cb], writes=[eb])
            db = Buf()
            ctx.dma("sp", self.escr, ev[0:8, :], reads=[eb], writes=[db])
            E_sb = ctx.tile([8, 2, 128], F32)
            ebf = Buf()
            hk = [ctx.tile([128], F32) for _ in range(2)]
            hb = [Buf(), Buf()]
            n = 0
            for h in range(8):
                for dist in range(2):
                    s_ = n % 2
                    n += 1
                    ctx.dma("sp", hk[s_], bass.AP(self.escr.tensor, h * 384 + 128 * dist, [[1, 128], [1, 128]]),
                            reads=[db], writes=[hb[s_]])
                    ctx.op("pe", lambda e: e.matmul(ps[:, 0:128], cst[:, C_J:C_J + 128], hk[s_], start=True, stop=True),
                           reads=[hb[s_], cb], writes=[pb])
                    ctx.op("dve", lambda e: e.tensor_copy(E_sb[:, h, dist, :], ps[:, 0:128]), reads=[pb], writes=[ebf])
            ctx.dma("sp", self.ebig, E_sb.rearrange("p a b c -> p (a b c)"), reads=[ebf])
            ctx.barrier()
        ctx.release(m0)

    def rms_rstd(self, x_sb, xbuf, C, sq, sqb, rstd, rstdb, ps_ap, psb, eps=RMS_EPS):
        ctx = self.ctx
        ctx.op("act", lambda e: e.activation(out=sq, in_=x_sb, func=AF.Square), reads=[xbuf], writes=[sqb])
        n = C
        while n > 1:
            hl = n // 2
            ctx.op("dve", lambda e: e.tensor_tensor(out=sq[:, 0:hl, :], in0=sq[:, 0:hl, :], in1=sq[:, hl:2 * hl, :], op=ALU.add),
                   reads=[sqb], writes=[sqb])
            n = hl
        ctx.op("pe", lambda e: e.matmul(ps_ap, self.ones_mean, sq[:, 0, :], start=True, stop=True),
               reads=[sqb, self.b_const], writes=[psb])
        ctx.op("act", lambda e: e.activation(out=rstd, in_=ps_ap, func=AF.Ln, bias=eps, scale=1.0), reads=[psb], writes=[rstdb])
        ctx.op("act", lambda e: e.activation(out=rstd, in_=rstd, func=AF.Exp, scale=-0.5), reads=[rstdb], writes=[rstdb])

    def phase_proj(self, kind, l, h_src):
        ctx, nc = self.ctx, self.nc
        m0 = ctx.mark()
        if kind == "A":
            wsrc, NW, gi = self.a_w_qkv[l], 3072, G_A_PRE + l
            fm = [(m * 128, 0.125 if m < 8 else 1.0) for m in range(16)]
            fm_dst, vofs, v_dst = self.qkT, 2048, self.vA
        elif kind == "B":
            wsrc, NW, gi = self.b_w_q[l - NA], 1024, G_B_PRE + (l - NA)
            fm = [(m * 128, 0.125) for m in range(8)]
            fm_dst, vofs, v_dst = self.qkT, None, None
        else:
            wsrc, NW, gi = self.w_kvf, 2064, G_KV
            fm = [(m * 128, 1.0) for m in range(8)]
            fm_dst, vofs, v_dst = self.kTs, 1024, self.vB
        nfm = len(fm)
        TT = 512
        W = ctx.tile([8, NW], BF16)
        nblk = (NW + 511) // 512
        wb = [Buf() for _ in range(nblk)]
        wv = wsrc.rearrange("(c p) n -> p c n", p=128)
        for b_ in range(nblk):
            c0_, c1_ = b_ * 512, min(NW, (b_ + 1) * 512)
            ctx.dma("pool", W[:, :, c0_:c1_], wv[:, :, c0_:c1_], writes=[wb[b_]])
        h_sb = [ctx.tile([8, TT], F32) for _ in range(2)]
        hb = [Buf(), Buf()]
        sq = ctx.tile([8, TT], F32)
        sqb = Buf()
        rstd = ctx.tile([TT], F32)
        rstdb = Buf()
        xn2 = [ctx.tile([8, TT], BF16) for _ in range(2)]
        xnb2 = [Buf(), Buf()]
        st = [ctx.tile([nfm, TT], BF16) for _ in range(2)]
        stb = [Buf(), Buf()]
        if vofs is not None:
            vst = [ctx.tile([8, 4, 128], BF16) for _ in range(2)]
            vstb = [Buf(), Buf()]
        if kind == "KVF":
            lfp = ctx.tile([S], F32)
            lfb = Buf()
            ftmp = ctx.tile([TT], F32)
            ftb = Buf()
        hview = h_src.rearrange("(c p) t -> p c t", p=128)
        with nc.psum_tensor(self.uname("ps_proj"), [128, 8, 512], F32) as ps:
            ring = PsumRing([ps[:, i, :] for i in range(7)])
            ps_stat, psb_stat = ps[:, 7, :], Buf()
            NTL = S // TT

            def load_h(t_):
                ctx.dma("sp", h_sb[t_ % 2], hview[:, :, t_ * TT:(t_ + 1) * TT], writes=[hb[t_ % 2]])

            def prenorm(t_):
                z_ = t_ % 2
                self.rms_rstd(h_sb[z_], hb[z_], 8, sq, sqb, rstd, rstdb, ps_stat, psb_stat)
                for c in range(8):
                    ctx.op("dve", lambda e: e.scalar_tensor_tensor(out=xn2[z_][:, c, :], in0=h_sb[z_][:, c, :], scalar=self.gain_sb[:, gi, c:c + 1],
                                                                   in1=rstd, op0=ALU.mult, op1=ALU.mult),
                           reads=[hb[z_], rstdb], writes=[xnb2[z_]])
            load_h(0)
            load_h(1)
            prenorm(0)
            for tt in range(NTL):
                s_ = tt % 2
                xn, xnb = xn2[s_], xnb2[s_]
                if tt + 2 < NTL:
                    load_h(tt + 2)
                if tt + 1 < NTL:
                    prenorm(tt + 1)
                for m, (col, sc) in enumerate(fm):
                    pa, pbuf = ring.next()

                    def mm(e, pa=pa, col=col):
                        for c in range(8):
                            ins = e.matmul(pa, W[:, c, col:col + 128], xn[:, c, :], start=(c == 0), stop=(c == 7))
                        return ins
                    ctx.op("pe", mm, reads=[xnb, wb[col // 512]], writes=[pbuf])
                    if m % 2 == 0:
                        ctx.op("act", lambda e: e.activation(out=st[s_][:, m, :], in_=pa, func=AF.Copy, scale=sc), reads=[pbuf], writes=[stb[s_]])
                    else:
                        ctx.op("dve", lambda e: e.tensor_scalar(st[s_][:, m, :], pa, sc, None, op0=ALU.mult), reads=[pbuf], writes=[stb[s_]])
                ctx.dma("pool", fm_dst[0:nfm, :, tt * TT:(tt + 1) * TT].rearrange("m p t -> p m t"), st[s_], reads=[stb[s_]])
                if vofs is not None:
                    for tb in range(4):
                        for half in range(2):
                            pa, pbuf = ring.next()

                            def mm(e, pa=pa, tb=tb, half=half):
                                for c in range(8):
                                    ins = e.matmul(pa, xn[:, c, tb * 128:(tb + 1) * 128],
                                                   W[:, c, vofs + half * 512:vofs + (half + 1) * 512], start=(c == 0), stop=(c == 7))
                                return ins
                            ctx.op("pe", mm, reads=[xnb, wb[(vofs + half * 512) // 512]], writes=[pbuf])
                            src = pa.rearrange("p (g d) -> p g d", g=4)
                            dst = vst[s_][:, half * 4:(half + 1) * 4, tb, :]
                            if (tb + half) % 2 == 0:
                                ctx.op("act", lambda e: e.activation(out=dst, in_=src, func=AF.Copy), reads=[pbuf], writes=[vstb[s_]])
                            else:
                                ctx.op("dve", lambda e: e.tensor_copy(dst, src), reads=[pbuf], writes=[vstb[s_]])
                    ctx.dma("pool", v_dst[:, :, tt * 4:(tt + 1) * 4, :].rearrange("g p b d -> p g b d"), vst[s_], reads=[vstb[s_]])
                if kind == "KVF":
                    pa, pbuf = ring.next()

                    def mm(e, pa=pa):
                        for c in range(8):
                            ins = e.matmul(pa[0:16, :], W[:, c, 2048:2064], xn[:, c, :], start=(c == 0), stop=(c == 7))
                        return ins
                    ctx.op("pe", mm, reads=[xnb, wb[2048 // 512]], writes=[pbuf])
                    ctx.op("act", lambda e: e.activation(out=ftmp[0:16, :], in_=pa[0:16, :], func=AF.Exp, bias=self.negbf[0:16, 0:1], scale=-1.0),
                           reads=[pbuf, self.b_const], writes=[ftb])
                    ctx.op("act", lambda e: e.activation(out=lfp[0:16, tt * TT:(tt + 1) * TT], in_=ftmp[0:16, :], func=AF.Ln, bias=1.0, scale=1.0),
                           reads=[ftb], writes=[lfb])
            if kind == "KVF":
                zer = ctx.tile([S], F32)
                cneg = ctx.tile([S], F32)
                zb, cb_ = Buf(), Buf()
                ctx.op("pool", lambda e: e.memset(zer[0:16, :], 0.0), writes=[zb])
                ctx.op("dve", lambda e: e.tensor_tensor_scan(cneg[0:16, :], lfp[0:16, :], zer[0:16, :], 0.0, op0=ALU.add, op1=ALU.add),
                       reads=[lfb, zb], writes=[cb_])
                pa, pbuf = ring.next()

                def tr(e, pa=pa):
                    for kb in range(32):
                        ins = e.transpose(pa[:, kb * 16:(kb + 1) * 16], cneg[0:16, kb * 128:(kb + 1) * 128], self.identf[0:16, 0:16])
                    return ins
                ctx.op("pe", tr, reads=[cb_, self.b_const], writes=[pbuf])
                ctx.op("dve", lambda e: e.tensor_copy(self.cnegT.rearrange("p a b -> p (a b)"), pa), reads=[pbuf], writes=[self.b_const])
                dblk = ctx.tile([16, 8], F32)
                dbb = Buf()
                crefv = cneg.rearrange("p (q t) -> p q t", t=512)[0:16, :, 0]
                for hh in range(16):
                    ctx.op("dve", lambda e: e.tensor_scalar(dblk[0:16, hh, :], crefv, self.identf[0:16, hh:hh + 1], None, op0=ALU.mult),
                           reads=[cb_, self.b_const], writes=[dbb])
                pa, pbuf = ring.next()
                ctx.op("pe", lambda e: e.matmul(pa[:, 0:128], self.ones_f[0:16, :], dblk[0:16, :, :].rearrange("p a b -> p (a b)"), start=True, stop=True),
                       reads=[dbb, self.b_const], writes=[pbuf])
                ctx.op("dve", lambda e: e.tensor_copy(self.cref.rearrange("p a b -> p (a b)"), pa[:, 0:128]), reads=[pbuf], writes=[self.b_const])
            ctx.barrier()
        ctx.release(m0)

    def phase_att(self, mode, l):
        ctx, nc = self.ctx, self.nc
        m0 = ctx.mark()
        A = mode == "A"
        NU = 8
        q_sb = [ctx.tile([S], BF16) for _ in range(2)]
        k_sb = [ctx.tile([S], BF16) for _ in range(2)]
        v_sb = [ctx.tile([32, 130], BF16) for _ in range(2)]
        qb = [Buf(), Buf()]
        kb_ = [Buf(), Buf()]
        vb = [Buf(), Buf()]
        NP = 4
        p_sb = [ctx.tile([2, 512], BF16) for _ in range(NP)]
        pbf = [Buf() for _ in range(NP)]
        rc = ctx.tile([2, 4], F32)
        nl = ctx.tile([4], F32)
        o1 = ctx.tile([4, 128], F32)
        o_sb = ctx.tile([4, 128], F32)
        sqo = ctx.tile([4, 128], F32)
        ss = ctx.tile([4], F32)
        rs4 = ctx.tile([4], F32)
        on_bf = ctx.tile([4, 128], BF16)
        oT_st = [ctx.tile([512], BF16) for _ in range(2)]
        ostb = [Buf(), Buf()]
        biasq = ctx.tile([32, 2], F32)
        wq = ctx.tile([32, 2], F32)
        bqb = Buf()
        wqb = Buf()
        NV = 4
        vs_sb = [ctx.tile([2, 65], BF16) for _ in range(NV)]
        vsb = [Buf() for _ in range(NV)]
        Eb = Buf()
        if A:
            E_sb = ctx.tile([8, 2, 128], F32)
            ctx.dma("sp", E_sb.rearrange("p a b c -> p (a b c)"), self.ebig, writes=[Eb])
        acc_sb = [ctx.tile([3, 512], F32) for _ in range(2)]
        acb = [Buf(), Buf()]
        gstep = [0]
        dq = []

        def defer(k, fn):
            dq.append((gstep[0] + k, fn))

        def run_deferred(force=False):
            while dq and (force or dq[0][0] <= gstep[0]):
                dq.pop(0)[1]()
        eb = Buf()
        onb = Buf()
        for s_ in range(2):
            if A:
                ctx.op("dve", lambda e: e.memset(v_sb[s_][:, :, 128:130], 1.0), writes=[vb[s_]])
            else:
                v4 = v_sb[s_].rearrange("p b (h d) -> p b h d", h=2)
                ctx.op("dve", lambda e: e.memset(v4[:, :, :, 64:65], 1.0), writes=[vb[s_]])

        def load_unit(u):
            s_ = u % 2
            if A:
                ctx.dma("sp", q_sb[s_], self.qkT[u], writes=[qb[s_]])
                ctx.dma("sp", k_sb[s_], self.qkT[8 + u], writes=[kb_[s_]])
                ctx.dma("sp", v_sb[s_][:, :, 0:128], self.vA[u], writes=[vb[s_]])
            else:
                ctx.dma("sp", q_sb[s_], self.qkT[u], writes=[qb[s_]])
                ctx.dma("sp", k_sb[s_], self.kTs[u], writes=[kb_[s_]])
                v4 = v_sb[s_].rearrange("p b (h d) -> p b h d", h=2)
                ctx.dma("sp", v4[:, :, :, 0:64], self.vB[u].rearrange("p b (h d) -> p b h d", h=2), writes=[vb[s_]])

        with nc.psum_tensor(self.uname("ps_s"), [128, 2, 2, 512], F32) as ps_s, \
                nc.psum_tensor(self.uname("ps_acc"), [128, 3, 512], F32) as ps_acc, \
                nc.psum_tensor(self.uname("ps_t"), [128, 1024], BF16) as ps_t:
            psb = [Buf(), Buf()]
            accb = Buf()
            ptb = Buf()
            W_ = 129 if A else 65

            def acc(c, j):
                a = c * 4 + j
                per = 3 if A else 7
                bnk, pos = a // per, a % per
                return ps_acc[:, bnk, pos * W_:(pos + 1) * W_]

            load_unit(0)
            for u in range(NU):
                s_ = u % 2
                if u + 1 < NU:
                    load_unit(u + 1)
                steps = [(qt, kb) for qt in range(8) for kb in range(4 * qt + 4)]
                nst = len(steps)

                def emit_S(i):
                    qt, kb = steps[i]
                    jmin = max(0, kb - 4 * qt)
                    c0 = jmin * 128
                    sl = i % 2

                    def mm(e):
                        e.matmul(ps_s[:, sl, 0, c0:512], k_sb[s_][0:64, kb * 128:(kb + 1) * 128], q_sb[s_][0:64, qt * 512 + c0:(qt + 1) * 512],
                                 start=True, stop=True, tile_position=(0, 0))
                        return e.matmul(ps_s[:, sl, 1, c0:512], k_sb[s_][64:128, kb * 128:(kb + 1) * 128], q_sb[s_][64:128, qt * 512 + c0:(qt + 1) * 512],
                                        start=True, stop=True, tile_position=(64, 0))
                    ctx.op("pe", mm, reads=[qb[s_], kb_[s_]], writes=[psb[sl]])
                    pi = i % NP
                    ctx.op("act", lambda e: e.activation(out=p_sb[pi][:, :, c0:512], in_=ps_s[:, sl, :, c0:512], func=AF.Exp),
                           reads=[psb[sl]], writes=[pbf[pi]])
                    j0 = kb - 4 * qt
                    if A and -1 <= j0 <= 3:
                        ja, jb = max(j0, 0), min(j0 + 1, 3)
                        da = ja - j0
                        nbk = jb - ja + 1
                        pv_ = p_sb[pi][:, :, ja * 128:(jb + 1) * 128].rearrange("p c (d q) -> p c d q", d=nbk)
                        ev_ = E_sb[:, u, da:da + nbk, :].unsqueeze(1).to_broadcast([128, 2, nbk, 128])
                        ctx.op("dve", lambda e: e.tensor_tensor(out=pv_, in0=pv_, in1=ev_, op=ALU.mult),
                               reads=[pbf[pi], Eb], writes=[pbf[pi]])
                    if (not A) and 0 <= j0 <= 3:
                        pv_ = p_sb[pi][:, :, j0 * 128:(j0 + 1) * 128]
                        ev_ = self.tri.unsqueeze(1).to_broadcast([128, 2, 128])
                        ctx.op("dve", lambda e: e.tensor_tensor(out=pv_, in0=pv_, in1=ev_, op=ALU.mult),
                               reads=[pbf[pi], self.b_const], writes=[pbf[pi]])

                def emit_VS(i):
                    qt, kb = steps[i]
                    if kb == 0:
                        for hh in range(2):
                            hd = 2 * u + hh
                            ctx.op("dve", lambda e: e.tensor_scalar(biasq[:, :, hh], self.cnegT[:, :, hd], self.cref[:, hd, qt:qt + 1], None, op0=ALU.subtract),
                                   reads=[self.b_const], writes=[bqb])
                        ctx.op("act", lambda e: e.activation(out=wq, in_=biasq, func=AF.Exp), reads=[bqb], writes=[wqb])
                    vi = i % NV
                    v4_ = v_sb[s_].rearrange("p b (h d) -> p b h d", h=2)
                    ctx.op("dve", lambda e: e.tensor_tensor(out=vs_sb[vi], in0=v4_[:, kb, :, :],
                                                            in1=wq[:, kb, :].unsqueeze(2).to_broadcast([128, 2, 65]), op=ALU.mult),
                           reads=[vb[s_], wqb], writes=[vsb[vi]])

                def emit_PV(i):
                    qt, kb = steps[i]
                    jmin = max(0, kb - 4 * qt)
                    pi = i % NP

                    def mm(e):
                        ins = None
                        started = set()
                        per = 3 if A else 7
                        for j in range(jmin, 4):
                            for c in range(2):
                                if A:
                                    rhs = v_sb[s_][:, kb, 0:129]
                                else:
                                    rhs = vs_sb[i % NV][:, c, :]
                                bnk = (c * 4 + j) // per
                                st_ = (kb == 0) and (bnk not in started)
                                started.add(bnk)
                                ins = e.matmul(acc(c, j), p_sb[pi][:, c, j * 128:(j + 1) * 128], rhs,
                                               start=st_, stop=(kb == 4 * qt + j), skip_group_check=True)
                        return ins
                    ctx.op("pe", mm, reads=[pbf[pi], vb[s_]] + ([] if A else [vsb[i % NV]]), writes=[accb])
                    gstep[0] += 1
                    run_deferred()
                    if kb == 4 * qt + 3:
                        emit_epi(qt)

                def emit_epi(qt, u=u):
                    run_deferred(force=True)
                    so = (u * 8 + qt) % 2
                    ea = (u * 8 + qt) % 2
                    nb = 3 if A else 2
                    for b_ in range(nb):
                        ctx.op("dve", lambda e: e.tensor_copy(acc_sb[ea][:, b_, :], ps_acc[:, b_, :]), reads=[accb], writes=[acb[ea]])

                    def sacc(c, j):
                        a = c * 4 + j
                        per = 3 if A else 7
                        bnk, pos = a // per, a % per
                        return acc_sb[ea][:, bnk, pos * W_:(pos + 1) * W_]
                    for c in range(2):
                        for j in range(4):
                            ctx.op("dve", lambda e: e.reciprocal(rc[:, c, j:j + 1], sacc(c, j)[:, W_ - 1:W_]), reads=[acb[ea]], writes=[eb])
                    if A:
                        ctx.op("dve", lambda e: e.tensor_scalar(nl, rc[:, 1, :], self.neglam[:, l:l + 1], None, op0=ALU.mult), reads=[eb, self.b_const], writes=[eb])
                        for j in range(4):
                            ctx.op("dve", lambda e: e.tensor_scalar(o1[:, j, :], sacc(0, j)[:, 0:128], rc[:, 0, j:j + 1], None, op0=ALU.mult),
                                   reads=[acb[ea], eb], writes=[eb])
                            ctx.op("dve", lambda e: e.scalar_tensor_tensor(out=o_sb[:, j, :], in0=sacc(1, j)[:, 0:128], scalar=nl[:, j:j + 1], in1=o1[:, j, :],
                                                                           op0=ALU.mult, op1=ALU.add), reads=[acb[ea], eb], writes=[eb])
                        ctx.op("dve", lambda e: e.tensor_tensor(out=sqo, in0=o_sb, in1=o_sb, op=ALU.mult), reads=[eb], writes=[eb])
                        ctx.op("dve", lambda e: e.reduce_sum(out=ss, in_=sqo, axis=AX.X), reads=[eb], writes=[eb])

                        def e2():
                            ctx.op("act", lambda e: e.activation(out=rs4, in_=ss, func=AF.Ln, bias=SUBLN_EPS, scale=1.0 / 128), reads=[eb], writes=[eb])
                            ctx.op("act", lambda e: e.activation(out=rs4, in_=rs4, func=AF.Exp, scale=-0.5), reads=[eb], writes=[eb])

                        def e3():
                            for j in range(4):
                                ctx.op("dve", lambda e: e.scalar_tensor_tensor(out=on_bf[:, j, :], in0=o_sb[:, j, :], scalar=rs4[:, j:j + 1], in1=self.gsub[:, l, :],
                                                                               op0=ALU.mult, op1=ALU.mult), reads=[eb, self.b_const], writes=[onb])
                        defer(6, e2)
                        defer(9, e3)
                    else:
                        for j in range(4):
                            for c in range(2):
                                ctx.op("dve", lambda e: e.tensor_scalar(on_bf[:, j, c * 64:(c + 1) * 64], sacc(c, j)[:, 0:64], rc[:, c, j:j + 1], None, op0=ALU.mult),
                                       reads=[acb[ea], eb], writes=[onb])

                    def e4():
                        def tr(e):
                            for j in range(4):
                                ins = e.transpose(ps_t[:, j * 128:(j + 1) * 128], on_bf[:, j, :], self.ident_bf)
                            return ins
                        ctx.op("pe", tr, reads=[onb, self.b_const], writes=[ptb])
                        ctx.op("dve", lambda e: e.tensor_copy(oT_st[so], ps_t[:, 0:512]), reads=[ptb], writes=[ostb[so]])
                        ctx.dma("pool", self.oT[u * 128:(u + 1) * 128, qt * 512:(qt + 1) * 512], oT_st[so], reads=[ostb[so]])
                    defer(12 if A else 6, e4)

                emit_S(0)
                if not A:
                    for v_ in range(min(3, nst)):
                        emit_VS(v_)
                for i in range(nst):
                    if i + 1 < nst:
                        emit_S(i + 1)
                    if (not A) and i + 3 < nst:
                        emit_VS(i + 3)
                    emit_PV(i)
            run_deferred(force=True)
            ctx.barrier()
        ctx.release(m0)

    def phase_po(self, l, h_src, w_o, gi):
        ctx, nc = self.ctx, self.nc
        m0 = ctx.mark()
        TT = 512
        W = ctx.tile([8, D], BF16)
        wb = [Buf() for _ in range(8)]
        for c in range(8):
            ctx.dma("pool", W[:, c, :], w_o[c * 128:(c + 1) * 128, :], writes=[wb[c]])
        o_sb = [ctx.tile([8, TT], BF16) for _ in range(2)]
        ob = [Buf(), Buf()]
        h_sb = [ctx.tile([8, TT], F32) for _ in range(2)]
        hb = [Buf(), Buf()]
        a_sb2 = [ctx.tile([8, TT], F32) for _ in range(2)]
        ab2 = [Buf(), Buf()]
        sq = ctx.tile([8, TT], F32)
        sqb = Buf()
        rstd = ctx.tile([TT], F32)
        rstdb = Buf()
        hview = h_src.rearrange("(c p) t -> p c t", p=128)
        oview = self.oT.rearrange("(c p) t -> p c t", p=128)
        dview = self.hbuf.rearrange("(c p) t -> p c t", p=128)
        with nc.psum_tensor(self.uname("ps_po"), [128, 8, 512], F32) as ps:
            ring = PsumRing([ps[:, i, :] for i in range(7)])
            ps_stat, psb_stat = ps[:, 7, :], Buf()
            NTL = S // TT

            def load(tt):
                s_ = tt % 2
                ctx.dma("sp", o_sb[s_], oview[:, :, tt * TT:(tt + 1) * TT], writes=[ob[s_]])
                ctx.dma("sp", h_sb[s_], hview[:, :, tt * TT:(tt + 1) * TT], writes=[hb[s_]])

            def mmpart(tt, m_lo, m_hi):
                s_ = tt % 2
                a_sb, ab = a_sb2[s_], ab2[s_]
                evs = []
                for m in range(m_lo, m_hi):
                    pa, pbuf = ring.next()

                    def mm(e, pa=pa, m=m):
                        for c in range(8):
                            ins = e.matmul(pa, W[:, c, m * 128:(m + 1) * 128], o_sb[s_][:, c, :], start=(c == 0), stop=(c == 7))
                        return ins
                    ctx.op("pe", mm, reads=[ob[s_]] + wb, writes=[pbuf])

                    def ev(pa=pa, pbuf=pbuf, m=m):
                        if m % 2 == 0:
                            ctx.op("act", lambda e: e.activation(out=a_sb[:, m, :], in_=pa, func=AF.Copy), reads=[pbuf], writes=[ab])
                        else:
                            ctx.op("dve", lambda e: e.tensor_copy(a_sb[:, m, :], pa), reads=[pbuf], writes=[ab])
                    evs.append(ev)
                return evs

            def post1(tt):
                s_ = tt % 2
                self.rms_rstd(a_sb2[s_], ab2[s_], 8, sq, sqb, rstd, rstdb, ps_stat, psb_stat)

            def post2(tt):
                s_ = tt % 2
                a_sb, ab = a_sb2[s_], ab2[s_]
                for c in range(8):
                    ctx.op("dve", lambda e: e.scalar_tensor_tensor(out=a_sb[:, c, :], in0=a_sb[:, c, :], scalar=self.gain_sb[:, gi, c:c + 1], in1=rstd,
                                                                   op0=ALU.mult, op1=ALU.mult), reads=[ab, rstdb], writes=[ab])
                ctx.op("dve", lambda e: e.tensor_tensor(out=h_sb[s_], in0=h_sb[s_], in1=a_sb, op=ALU.add), reads=[ab, hb[s_]], writes=[hb[s_]])
                ctx.dma("pool", dview[:, :, tt * TT:(tt + 1) * TT], h_sb[s_], reads=[hb[s_]])

            load(0)
            for ev in mmpart(0, 0, 4):
                ev()
            for ev in mmpart(0, 4, 8):
                ev()
            for tt in range(NTL):
                if tt + 1 < NTL:
                    load(tt + 1)
                    evs = mmpart(tt + 1, 0, 4)
                    post1(tt)
                    for ev in evs:
                        ev()
                    for ev in mmpart(tt + 1, 4, 8):
                        ev()
                    post2(tt)
                else:
                    post1(tt)
                    post2(tt)
            ctx.barrier()
        ctx.release(m0)

    def phase_ffn(self, l, h_dst):
        ctx, nc = self.ctx, self.nc
        m0 = ctx.mark()
        TT = 256
        NTL = S // TT
        KPRE = 3
        Wi = ctx.tile([8, 2 * F], BF16)
        wib = {}
        wiv = self.ffn_w_in[l].rearrange("(c p) n -> p c n", p=128)
        for blk in range(NFC // 2):
            for gv in range(2):
                c0_ = gv * F + blk * 256
                wib[(gv, blk)] = Buf()
                ctx.dma("pool", Wi[:, :, c0_:c0_ + 256], wiv[:, :, c0_:c0_ + 256], writes=[wib[(gv, blk)]])
        Wo = ctx.tile([NFC, D], BF16)
        wob = Buf()
        wov = self.ffn_w_out[l].rearrange("(c p) n -> p c n", p=128)
        for c0 in range(0, NFC, 6):
            c1 = min(NFC, c0 + 6)
            ctx.dma("pool", Wo[:, c0:c1, :], wov[:, c0:c1, :], writes=[wob])
        cw = ctx.tile([4, 44], F32)
        cwb = Buf()
        ctx.dma("sp", cw.rearrange("p a b -> p (a b)"), self.convp[:, l * 176:(l + 1) * 176], writes=[cwb])
        h_sb = [ctx.tile([8, TT], F32) for _ in range(2)]
        hb = [Buf(), Buf()]
        sq = ctx.tile([8, TT], F32)
        sqb = Buf()
        rstd = ctx.tile([TT], F32)
        rstdb = Buf()
        xn = [ctx.tile([8, TT], BF16) for _ in range(2)]
        xnb = [Buf(), Buf()]
        NU_, NY_ = 3, 4
        u_sb = [ctx.tile([2, TT + 2], F32) for _ in range(NU_)]
        ub = [Buf() for _ in range(NU_)]
        uhb = [Buf() for _ in range(NU_)]
        y_sb = [ctx.tile([2, TT], F32) for _ in range(NY_)]
        yb = [Buf() for _ in range(NY_)]
        halo = ctx.tile([NFC, 2, 2], F32)
        halob = [Buf() for _ in range(NFC)]
        g_sb = ctx.tile([NFC, TT], BF16)
        gb = [Buf() for _ in range(NFC)]
        f_sb = ctx.tile([8, TT], F32)
        fb = Buf()
        rstd2 = ctx.tile([TT], F32)
        rstd2b = Buf()
        hview = self.hbuf.rearrange("(c p) t -> p c t", p=128)
        dview = h_dst.rearrange("(c p) t -> p c t", p=128)
        ctx.op("pool", lambda e: e.memset(halo, 0.0), writes=halob)
        with nc.psum_tensor(self.uname("ps_ffn"), [128, 8, 512], F32) as ps:
            uring = PsumRing([ps[:, i, :].rearrange("p (a t) -> p a t", a=2) for i in range(5)])
            fps = [ps[:, 5 + i, :].rearrange("p (a t) -> p a t", a=2) for i in range(2)]
            fpb = [Buf() for _ in range(2)]
            ps_stat, psb_stat = ps[:, 7, 0:TT], Buf()
            cnt = [0]

            def load(tt):
                s_ = tt % 2
                ctx.dma("sp", h_sb[s_], hview[:, :, tt * TT:(tt + 1) * TT], writes=[hb[s_]])

            def prenorm(tt):
                s_ = tt % 2
                self.rms_rstd(h_sb[s_], hb[s_], 8, sq, sqb, rstd, rstdb, ps_stat, psb_stat)
                for c in range(8):
                    ctx.op("dve", lambda e: e.scalar_tensor_tensor(out=xn[s_][:, c, :], in0=h_sb[s_][:, c, :], scalar=self.gain_sb[:, G_F_PRE + l, c:c + 1],
                                                                   in1=rstd, op0=ALU.mult, op1=ALU.mult),
                           reads=[hb[s_], rstdb], writes=[xnb[s_]])

            def pair(tt, i):
                s_ = tt % 2
                n = cnt[0]
                cnt[0] += 1
                us, ys = n % NU_, n % NY_
                pa, pbuf = uring.next()

                def mm(e):
                    for gv, col in ((0, i * 128), (1, F + i * 128)):
                        for c in range(8):
                            ins = e.matmul(pa[:, gv, :], Wi[:, c, col:col + 128], xn[s_][:, c, :], start=(c == 0), stop=(c == 7))
                    return ins
                ctx.op("pe", mm, reads=[xnb[s_], wib[(0, i // 2)], wib[(1, i // 2)]], writes=[pbuf])
                chs = (i, i + NFC)
                ctx.op("pool", lambda e: e.tensor_copy(u_sb[us][:, :, 0:2], halo[:, i, :, :]), reads=[halob[i]], writes=[uhb[us]])
                ctx.op("act", lambda e: e.activation(out=u_sb[us][:, :, 2:TT + 2], in_=pa, func=AF.Copy), reads=[pbuf], writes=[ub[us]])
                for gv in range(2):
                    ch = chs[gv]
                    ctx.op("act", lambda e: e.activation(out=y_sb[ys][:, gv, :], in_=pa[:, gv, :], func=AF.Identity,
                                                         scale=cw[:, 2, ch:ch + 1], bias=cw[:, 3, ch:ch + 1]),
                           reads=[pbuf, cwb], writes=[yb[ys]])
                ctx.op("pool", lambda e: e.tensor_copy(halo[:, i, :, :], u_sb[us][:, :, TT:TT + 2]), reads=[ub[us]], writes=[halob[i]])
                for tap, off in ((1, 1), (0, 0)):
                    for gv in range(2):
                        ch = chs[gv]
                        ctx.op("dve", lambda e: e.scalar_tensor_tensor(out=y_sb[ys][:, gv, :], in0=u_sb[us][:, gv, off:off + TT], scalar=cw[:, tap, ch:ch + 1],
                                                                       in1=y_sb[ys][:, gv, :], op0=ALU.mult, op1=ALU.add),
                               reads=[ub[us], uhb[us], yb[ys], cwb], writes=[yb[ys]])
                def stage_c():
                    ctx.op("act", lambda e: e.activation(out=y_sb[ys][:, 0, :], in_=y_sb[ys][:, 0, :], func=AF.Gelu_apprx_tanh), reads=[yb[ys]], writes=[yb[ys]])

                def stage_d():
                    ctx.op("pool", lambda e: e.tensor_tensor(out=g_sb[:, i, :], in0=y_sb[ys][:, 0, :], in1=y_sb[ys][:, 1, :], op=ALU.mult),
                           reads=[yb[ys]], writes=[gb[i]])
                return stage_c, stage_d

            def wout_part1(tt):
                for m in range(8):
                    k_ = (m // 2) % 2
                    pa, pbuf = fps[k_][:, m % 2, :], fpb[k_]

                    def mm(e, pa=pa, m=m):
                        for c in range(NFC):
                            ins = e.matmul(pa, Wo[:, c, m * 128:(m + 1) * 128], g_sb[:, c, :], start=(c == 0), stop=(c == NFC - 1))
                        return ins
                    ctx.op("pe", mm, reads=gb + [wob], writes=[pbuf])
                    if m % 2 == 1:
                        src = fps[k_]
                        dst = f_sb[:, m - 1:m + 1, :]
                        if (m // 2) % 2 == 0:
                            ctx.op("dve", lambda e: e.tensor_copy(dst, src), reads=[pbuf], writes=[fb])
                        else:
                            ctx.op("act", lambda e: e.activation(out=dst, in_=src, func=AF.Copy), reads=[pbuf], writes=[fb])
                ctx.op("act", lambda e: e.activation(out=sq, in_=f_sb, func=AF.Square), reads=[fb], writes=[sqb])
                n_ = 8
                while n_ > 1:
                    hl = n_ // 2
                    ctx.op("dve", lambda e: e.tensor_tensor(out=sq[:, 0:hl, :], in0=sq[:, 0:hl, :], in1=sq[:, hl:2 * hl, :], op=ALU.add),
                           reads=[sqb], writes=[sqb])
                    n_ = hl

            def wout_part2(tt):
                s_ = tt % 2
                ctx.op("pe", lambda e: e.matmul(ps_stat, self.ones_mean, sq[:, 0, :], start=True, stop=True), reads=[sqb, self.b_const], writes=[psb_stat])
                ctx.op("act", lambda e: e.activation(out=rstd2, in_=ps_stat, func=AF.Ln, bias=RMS_EPS, scale=1.0), reads=[psb_stat], writes=[rstd2b])
                ctx.op("act", lambda e: e.activation(out=rstd2, in_=rstd2, func=AF.Exp, scale=-0.5), reads=[rstd2b], writes=[rstd2b])
                for c in range(8):
                    ctx.op("dve", lambda e: e.scalar_tensor_tensor(out=f_sb[:, c, :], in0=f_sb[:, c, :], scalar=self.gain_sb[:, G_F_POST + l, c:c + 1], in1=rstd2,
                                                                   op0=ALU.mult, op1=ALU.mult), reads=[fb, rstd2b], writes=[fb])
                ctx.op("dve", lambda e: e.tensor_tensor(out=h_sb[s_], in0=h_sb[s_], in1=f_sb, op=ALU.add), reads=[fb, hb[s_]], writes=[hb[s_]])
                ctx.dma("sp", dview[:, :, tt * TT:(tt + 1) * TT], h_sb[s_], reads=[hb[s_]])

            LAG = 2
            pend = []

            def push(cd):
                pend.append(cd)
                while len(pend) > LAG:
                    c_, d_ = pend.pop(0)
                    c_()
                    d_()

            def flush(do_d=True):
                ds = []
                while pend:
                    c_, d_ = pend.pop(0)
                    c_()
                    if do_d:
                        d_()
                    else:
                        ds.append(d_)
                return ds

            load(0)
            prenorm(0)
            for i in range(KPRE):
                push(pair(0, i))
            for tt in range(NTL):
                for i in range(KPRE, NFC):
                    push(pair(tt, i))
                    if i == 6 and tt > 0:
                        wout_part2(tt - 1)
                    if i == 8 and tt + 1 < NTL:
                        load(tt + 1)
                    if i == 14 and tt + 1 < NTL:
                        prenorm(tt + 1)
                flush()
                ds = []
                if tt + 1 < NTL:
                    for i in range(KPRE):
                        pend.append(pair(tt + 1, i))
                    ds = flush(do_d=False)
                wout_part1(tt)
                for d_ in ds:
                    d_()
            wout_part2(NTL - 1)
            ctx.barrier()
        ctx.release(m0)


def _pack_small(inp):
    def pc(v):
        return np.ascontiguousarray(np.asarray(v, np.float32).reshape(8, 128).T)
    gl = []
    for name, n in (("a_norm_pre", 2), ("a_norm_post", 2)):
        gl += [pc(inp[name][i]) for i in range(n)]
    gl.append(pc(inp["kv_norm"]))
    for name, n in (("b_norm_pre", 2), ("b_norm_post", 2), ("ffn_norm_pre", 4), ("ffn_norm_post", 4)):
        gl += [pc(inp[name][i]) for i in range(n)]
    gains = np.ascontiguousarray(np.stack(gl, axis=1).reshape(128, NG * 8))
    cw = np.asarray(inp["ffn_conv_w"], np.float32)
    cb = np.asarray(inp["ffn_conv_b"], np.float32)
    allp = np.concatenate([cw, cb[:, None, :]], axis=1)
    convp = np.ascontiguousarray(allp.reshape(4, 4, 44, 128).transpose(3, 0, 1, 2).reshape(128, 4 * 4 * 44))
    lamv = np.stack([np.stack([inp["a_lam_q1"][l], inp["a_lam_k1"][l], inp["a_lam_q2"][l], inp["a_lam_k2"][l]]) for l in range(2)])
    lamv = np.ascontiguousarray(np.asarray(lamv, np.float32).reshape(1, 512))
    subln = np.ascontiguousarray(np.asarray(inp["a_subln"], np.float32).reshape(1, 256))
    b_f = np.ascontiguousarray(np.asarray(inp["b_f"], np.float32).reshape(16, 1))
    return gains, convp, lamv, subln, b_f


_PROG_CACHE = {}


def _get_prog(key, **kw):
    if key not in _PROG_CACHE:
        _PROG_CACHE[key] = Prog(**kw)
    return _PROG_CACHE[key]


def _in_maps(inp, xTs):
    gains, convp, lamv, subln, b_f = _pack_small(inp)
    cst = _make_consts()
    shared = {
        "rel_bias": np.ascontiguousarray(inp["rel_bias"], np.float32),
        "a_w_qkv": np.ascontiguousarray(inp["a_w_qkv"], np.float32),
        "a_w_o": np.ascontiguousarray(inp["a_w_o"], np.float32),
        "w_kvf": np.ascontiguousarray(inp["w_kvf"], np.float32),
        "b_w_q": np.ascontiguousarray(inp["b_w_q"], np.float32),
        "b_w_o": np.ascontiguousarray(inp["b_w_o"], np.float32),
        "ffn_w_in": np.ascontiguousarray(inp["ffn_w_in"], np.float32),
        "ffn_w_out": np.ascontiguousarray(inp["ffn_w_out"], np.float32),
        "gains": gains, "convp": convp, "lamv": lamv, "subln": subln, "b_f": b_f, "cst": cst,
    }
    return [dict(shared, xT=xTs[b]) for b in range(len(xTs))]


def kernel(**inp):
    x = np.asarray(inp["x"], np.float32)
    B = x.shape[0]
    xTs = [np.ascontiguousarray(x[b].T) for b in range(B)]
    prog = _get_prog("full")
    res = run_bass_kernel_spmd(prog.nc, _in_maps(inp, xTs), core_ids=list(range(B)))
    out = np.empty((B, S, D), np.float32)
    for b in range(B):
        out[b] = res.results[b]["yT"].T
    return out
```

```python
import math
from contextlib import ExitStack

import numpy as np
import concourse.bass as bass
import concourse.mybir as mybir
from concourse.bass_utils import run_bass_kernel_spmd

F32 = mybir.dt.float32
BF16 = mybir.dt.bfloat16
U8 = mybir.dt.uint8
AF = mybir.ActivationFunctionType
ALU = mybir.AluOpType
AX = mybir.AxisListType

S = 4096
D = 1024
DEPTH = 4
NA = 2
F = 2816
NFC = F // 128
RMS_EPS = 1e-6
SUBLN_EPS = 1e-5
N_BUCKETS = 32
MAX_DISTANCE = 128
ARENA_BYTES = 207 * 1024 + 512
DMA_RING = 8

G_A_PRE, G_A_POST, G_KV, G_B_PRE, G_B_POST, G_F_PRE, G_F_POST = 0, 2, 4, 5, 7, 9, 13
NG = 17


def lambda_init_fn(layer_idx):
    return 0.8 - 0.6 * math.exp(-0.3 * layer_idx)


class Buf:
    __slots__ = ("w", "r")

    def __init__(self):
        self.w = None
        self.r = {}


class Ctx:
    def __init__(self, nc, stack):
        self.nc = nc
        self.E = {"pe": nc.tensor, "act": nc.scalar, "dve": nc.vector, "pool": nc.gpsimd, "sp": nc.sync}
        self.semh = {}
        self.cnt = {}
        self.waited = {k: {} for k in self.E}
        for k in self.E:
            self.semh[k] = stack.enter_context(nc.semaphore("s_" + k))
            self.cnt[k] = 0
        self.dval = {}
        self.ringpos = {"sp": 0, "pool": 0}
        for q in ("sp", "pool"):
            for i in range(DMA_RING):
                key = "%sd%d" % (q, i)
                self.semh[key] = stack.enter_context(nc.semaphore("s_" + key))
                self.dval[key] = 0
        self.arena = stack.enter_context(nc.sbuf_tensor("arena", [128, ARENA_BYTES], U8))
        self.aoff = 0

    def mark(self):
        return self.aoff

    def release(self, m):
        self.aoff = m

    def tile(self, free_shape, dt):
        n = 1
        for s_ in free_shape:
            n *= s_
        sz = n * (4 if dt == F32 else 2)
        off = (self.aoff + 63) // 64 * 64
        assert off + sz <= ARENA_BYTES, "arena overflow %d" % (off + sz)
        self.aoff = off + sz
        v = self.arena[:, off:off + sz].bitcast(dt)
        if len(free_shape) == 2:
            v = v.rearrange("p (a b) -> p a b", a=free_shape[0])
        elif len(free_shape) == 3:
            v = v.rearrange("p (a b c) -> p a b c", a=free_shape[0], b=free_shape[1])
        return v

    def wait(self, eng, toks):
        wd = self.waited[eng]
        for (k, v) in toks:
            if k == eng:
                if eng == "pe":
                    continue
                if v < self.cnt[eng] - 2:
                    continue
            if v > wd.get(k, 0):
                self.E[eng].wait_ge(self.semh[k], v)
                wd[k] = v

    @staticmethod
    def _deps(reads, writes, extra):
        deps = list(extra)
        for b in reads:
            if b.w is not None:
                deps.append(b.w)
        for b in writes:
            if b.w is not None:
                deps.append(b.w)
            deps.extend(b.r.items())
        return deps

    @staticmethod
    def _mark(tok, reads, writes):
        for b in reads:
            if b.r.get(tok[0], 0) < tok[1]:
                b.r[tok[0]] = tok[1]
        for b in writes:
            b.w = tok
            b.r = {}

    def op(self, eng, fn, reads=(), writes=(), extra=()):
        self.wait(eng, self._deps(reads, writes, extra))
        ins = fn(self.E[eng])
        self.cnt[eng] += 1
        ins.then_inc(self.semh[eng], 1)
        tok = (eng, self.cnt[eng])
        self._mark(tok, reads, writes)
        return tok

    def dma(self, q, out, in_, reads=(), writes=(), extra=()):
        idx = self.ringpos[q] % DMA_RING
        self.ringpos[q] += 1
        key = "%sd%d" % (q, idx)
        deps = self._deps(reads, writes, extra)
        if self.dval[key] > 0:
            deps.append((key, self.dval[key]))
        self.wait(q, deps)
        ins = self.E[q].dma_start(out=out, in_=in_)
        self.dval[key] += 16
        ins.then_inc(self.semh[key], 16)
        tok = (key, self.dval[key])
        self._mark(tok, reads, writes)
        return tok

    def barrier(self):
        toks = [(k, self.cnt[k]) for k in self.E if self.cnt[k] > 0]
        toks += [(k, v) for k, v in self.dval.items() if v > 0]
        for eng in self.E:
            wd = self.waited[eng]
            for (k, v) in toks:
                if k == eng:
                    continue
                if v > wd.get(k, 0):
                    self.E[eng].wait_ge(self.semh[k], v)
                    wd[k] = v


class PsumRing:
    def __init__(self, aps):
        self.aps = aps
        self.bufs = [Buf() for _ in aps]
        self.i = 0

    def next(self):
        k = self.i % len(self.aps)
        self.i += 1
        return self.aps[k], self.bufs[k]


def _t5_bucket_np(d):
    max_exact = N_BUCKETS // 2
    d = np.maximum(d, 0)
    log_ratio = np.log(np.maximum(d, 1).astype(np.float32) / np.float32(max_exact)) / np.float32(
        math.log(MAX_DISTANCE / max_exact))
    large = np.minimum(max_exact + (log_ratio * (N_BUCKETS - max_exact)).astype(np.int32), N_BUCKETS - 1)
    return np.where(d < max_exact, d, large)


C_IDENT, C_J, C_TRI, C_OH, C_MK, C_NCOL = 0, 128, 256, 384, 768, 1152


def _make_consts():
    c = np.zeros((128, C_NCOL), np.float32)
    c[:, C_IDENT:C_IDENT + 128] = np.eye(128, dtype=np.float32)
    c[:, C_J:C_J + 128] = np.eye(128, dtype=np.float32)[::-1]
    k = np.arange(128)[:, None]
    q = np.arange(128)[None, :]
    c[:, C_TRI:C_TRI + 128] = (q >= k).astype(np.float32)
    i = np.arange(383)
    dd = i - 127
    bk = _t5_bucket_np(dd)
    for b in range(32):
        c[b, C_OH:C_OH + 383] = ((dd >= 0) & (bk == b)).astype(np.float32)
    c[0:8, C_MK:C_MK + 383] = (dd >= 0).astype(np.float32)[None, :]
    return c


class Prog:
    def __init__(self, layers=(0, 1, 2, 3), first_from_x=True, last_to_y=True, debug=False, stop=None):
        self.layers = tuple(layers)
        self.debug = debug
        nc = bass.Bass("TRN2", target_bir_lowering=False)
        self.nc = nc
        self.stack = ExitStack()
        ctx = Ctx(nc, self.stack)
        self.ctx = ctx

        def din(name, shape):
            return nc.dram_tensor(name, list(shape), F32, kind="ExternalInput").ap()

        self.xT = din("xT", [D, S])
        self.rel_bias = din("rel_bias", [32, 8])
        self.a_w_qkv = din("a_w_qkv", [2, D, 3 * D])
        self.a_w_o = din("a_w_o", [2, D, D])
        self.w_kvf = din("w_kvf", [D, 2064])
        self.b_w_q = din("b_w_q", [2, D, D])
        self.b_w_o = din("b_w_o", [2, D, D])
        self.ffn_w_in = din("ffn_w_in", [4, D, 2 * F])
        self.ffn_w_out = din("ffn_w_out", [4, F, D])
        self.gains = din("gains", [128, NG * 8])
        self.convp = din("convp", [128, 4 * 4 * 44])
        self.lamv = din("lamv", [1, 2 * 4 * 64])
        self.subln = din("subln", [1, 2 * 128])
        self.b_f = din("b_f", [16, 1])
        self.cst = din("cst", [128, C_NCOL])
        self.yT = nc.dram_tensor("yT", [D, S], F32, kind="ExternalOutput").ap()

        kind = "ExternalOutput" if debug else "Internal"

        def dscr(name, shape, dt):
            return nc.dram_tensor(name, list(shape), dt, kind=kind).ap()

        self.hbuf = dscr("hbuf", [D, S], F32)
        self.qkT = dscr("qkT", [16, 128, S], BF16)
        self.kTs = dscr("kTs", [8, 128, S], BF16)
        self.vA = dscr("vA", [8, 128, 32, 128], BF16)
        self.vB = dscr("vB", [8, 128, 32, 128], BF16)
        self.oT = dscr("oT", [D, S], BF16)
        self.escr = dscr("escr", [8, 384], F32)
        self.ebig = dscr("ebig", [128, 8 * 2 * 128], F32)

        self.setup()
        for l in self.layers:
            src = self.xT if l == self.layers[0] else self.hbuf
            if l < NA:
                self.phase_proj("A", l, src)
                if stop == "proj":
                    break
                self.phase_att("A", l)
                if stop == "att":
                    break
                self.phase_po(l, src, self.a_w_o[l], G_A_POST + l)
                if stop == "po":
                    break
            else:
                j = l - NA
                if l == NA:
                    self.phase_proj("KVF", l, src)
                self.phase_proj("B", l, src)
                self.phase_att("B", l)
                self.phase_po(l, src, self.b_w_o[j], G_B_POST + j)
            dst = self.yT if l == DEPTH - 1 else self.hbuf
            self.phase_ffn(l, dst)
        ctx.barrier()
        self.stack.close()

    def uname(self, base):
        self._uid = getattr(self, "_uid", 0) + 1
        return "%s_%d" % (base, self._uid)

    def setup(self):
        ctx, nc = self.ctx, self.nc
        self.identf = ctx.tile([128], F32)
        self.tri = ctx.tile([128], F32)
        self.ident_bf = ctx.tile([128], BF16)
        self.ones_mean = ctx.tile([128], F32)
        self.ones_f = ctx.tile([128], F32)
        self.gain_sb = ctx.tile([NG, 8], F32)
        self.neglam = ctx.tile([2], F32)
        self.gsub = ctx.tile([2, 128], F32)
        self.negbf = ctx.tile([1], F32)
        self.cnegT = ctx.tile([32, 16], F32)
        self.cref = ctx.tile([16, 8], F32)
        self.b_const = Buf()
        m0 = ctx.mark()
        cst = ctx.tile([C_NCOL], F32)
        cb = Buf()
        ctx.dma("sp", cst, self.cst, writes=[cb])
        gb = Buf()
        ctx.dma("sp", self.gain_sb.rearrange("p a b -> p (a b)"), self.gains, writes=[gb])
        ctx.dma("sp", self.negbf[0:16, :], self.b_f, writes=[gb])
        lam_sb = ctx.tile([8, 64], F32)
        lb = Buf()
        ctx.dma("sp", lam_sb.rearrange("p a b -> p (a b)"), bass.AP(self.lamv.tensor, 0, [[0, 128], [1, 512]]), writes=[lb])
        sub_sb = ctx.tile([2, 128], F32)
        ctx.dma("sp", sub_sb.rearrange("p a b -> p (a b)"), bass.AP(self.subln.tensor, 0, [[0, 128], [1, 256]]), writes=[lb])
        rb_sb = ctx.tile([8], F32)
        rb31 = ctx.tile([1], F32)
        ctx.dma("sp", rb_sb[0:32, :], self.rel_bias, writes=[lb])
        ctx.dma("sp", rb31[0:8, :], bass.AP(self.rel_bias.tensor, 31 * 8, [[1, 8], [1, 1]]), writes=[lb])
        k = self.b_const
        ctx.op("dve", lambda e: e.tensor_copy(self.identf, cst[:, C_IDENT:C_IDENT + 128]), reads=[cb], writes=[k])
        ctx.op("dve", lambda e: e.tensor_copy(self.tri, cst[:, C_TRI:C_TRI + 128]), reads=[cb], writes=[k])
        ctx.op("dve", lambda e: e.tensor_copy(self.ident_bf, cst[:, C_IDENT:C_IDENT + 128]), reads=[cb], writes=[k])
        ctx.op("dve", lambda e: e.memset(self.ones_mean, 1.0 / D), writes=[k])
        ctx.op("dve", lambda e: e.memset(self.ones_f, 1.0), writes=[k])
        ctx.op("dve", lambda e: e.tensor_scalar(self.negbf[0:16, :], self.negbf[0:16, :], -1.0, None, op0=ALU.mult), reads=[gb], writes=[k])
        prod = ctx.tile([4, 64], F32)
        ssum = ctx.tile([4], F32)
        tb = Buf()
        for l in range(2):
            ctx.op("dve", lambda e: e.tensor_tensor(out=prod[:, 0:2, :], in0=lam_sb[:, 4 * l:4 * l + 4:2, :],
                                                    in1=lam_sb[:, 4 * l + 1:4 * l + 4:2, :], op=ALU.mult), reads=[lb], writes=[tb])
            ctx.op("dve", lambda e: e.reduce_sum(out=ssum[:, 0:2], in_=prod[:, 0:2, :], axis=AX.X), reads=[tb], writes=[tb])
            ctx.op("act", lambda e: e.activation(out=ssum[:, 2:4], in_=ssum[:, 0:2], func=AF.Exp), reads=[tb], writes=[tb])
            ctx.op("dve", lambda e: e.tensor_tensor(out=ssum[:, 0:1], in0=ssum[:, 3:4], in1=ssum[:, 2:3], op=ALU.subtract), reads=[tb], writes=[tb])
            ctx.op("dve", lambda e: e.tensor_scalar(self.neglam[:, l:l + 1], ssum[:, 0:1], -lambda_init_fn(l), None, op0=ALU.add), reads=[tb], writes=[k])
            ctx.op("dve", lambda e: e.tensor_scalar(self.gsub[:, l, :], sub_sb[:, l, :], 1.0 - lambda_init_fn(l), None, op0=ALU.mult), reads=[lb], writes=[k])
        with nc.psum_tensor("ps_setup", [128, 512], F32) as ps:
            pb = Buf()
            ev = ctx.tile([384], F32)
            eb = Buf()
            ctx.op("dve", lambda e: e.tensor_scalar(rb31[0:8, :], rb31[0:8, :], -1.0, None, op0=ALU.mult), reads=[lb], writes=[lb])
            ctx.op("pe", lambda e: e.matmul(ps[0:8, 0:384], rb_sb[0:32, 0:8], cst[0:32, C_OH:C_OH + 384], start=True, stop=True),
                   reads=[lb, cb], writes=[pb])
            ctx.op("act", lambda e: e.activation(out=ev[0:8, :], in_=ps[0:8, 0:384], func=AF.Exp, bias=rb31[0:8, 0:1], scale=1.0),
                   reads=[pb, lb], writes=[eb])
            ctx.op("dve", lambda e: e.tensor_tensor(out=ev[0:8, :], in0=ev[0:8, :], in1=cst[0:8, C_MK:C_MK + 384], op=ALU.mult),
                   reads=[eb, cb], writes=[eb])
            db = Buf()
            ctx.dma("sp", self.escr, ev[0:8, :], reads=[eb], writes=[db])
            E_sb = ctx.tile([8, 2, 128], F32)
            ebf = Buf()
            hk = [ctx.tile([128], F32) for _ in range(2)]
            hb = [Buf(), Buf()]
            n = 0
            for h in range(8):
                for dist in range(2):
                    s_ = n % 2
                    n += 1
                    ctx.dma("sp", hk[s_], bass.AP(self.escr.tensor, h * 384 + 128 * dist, [[1, 128], [1, 128]]),
                            reads=[db], writes=[hb[s_]])
                    ctx.op("pe", lambda e: e.matmul(ps[:, 0:128], cst[:, C_J:C_J + 128], hk[s_], start=True, stop=True),
                           reads=[hb[s_], cb], writes=[pb])
                    ctx.op("dve", lambda e: e.tensor_copy(E_sb[:, h, dist, :], ps[:, 0:128]), reads=[pb], writes=[ebf])
            ctx.dma("sp", self.ebig, E_sb.rearrange("p a b c -> p (a b c)"), reads=[ebf])
            ctx.barrier()
        ctx.release(m0)

    def rms_rstd(self, x_sb, xbuf, C, sq, sqb, rstd, rstdb, ps_ap, psb, eps=RMS_EPS):
        ctx = self.ctx
        ctx.op("act", lambda e: e.activation(out=sq, in_=x_sb, func=AF.Square), reads=[xbuf], writes=[sqb])
        n = C
        while n > 1:
            hl = n // 2
            ctx.op("dve", lambda e: e.tensor_tensor(out=sq[:, 0:hl, :], in0=sq[:, 0:hl, :], in1=sq[:, hl:2 * hl, :], op=ALU.add),
                   reads=[sqb], writes=[sqb])
            n = hl
        ctx.op("pe", lambda e: e.matmul(ps_ap, self.ones_mean, sq[:, 0, :], start=True, stop=True),
               reads=[sqb, self.b_const], writes=[psb])
        ctx.op("act", lambda e: e.activation(out=rstd, in_=ps_ap, func=AF.Ln, bias=eps, scale=1.0), reads=[psb], writes=[rstdb])
        ctx.op("act", lambda e: e.activation(out=rstd, in_=rstd, func=AF.Exp, scale=-0.5), reads=[rstdb], writes=[rstdb])

    def phase_proj(self, kind, l, h_src):
        ctx, nc = self.ctx, self.nc
        m0 = ctx.mark()
        if kind == "A":
            wsrc, NW, gi = self.a_w_qkv[l], 3072, G_A_PRE + l
            fm = [(m * 128, 0.125 if m < 8 else 1.0) for m in range(16)]
            fm_dst, vofs, v_dst = self.qkT, 2048, self.vA
        elif kind == "B":
            wsrc, NW, gi = self.b_w_q[l - NA], 1024, G_B_PRE + (l - NA)
            fm = [(m * 128, 0.125) for m in range(8)]
            fm_dst, vofs, v_dst = self.qkT, None, None
        else:
            wsrc, NW, gi = self.w_kvf, 2064, G_KV
            fm = [(m * 128, 1.0) for m in range(8)]
            fm_dst, vofs, v_dst = self.kTs, 1024, self.vB
        nfm = len(fm)
        TT = 512
        W = ctx.tile([8, NW], BF16)
        nblk = (NW + 511) // 512
        wb = [Buf() for _ in range(nblk)]
        wv = wsrc.rearrange("(c p) n -> p c n", p=128)
        for b_ in range(nblk):
            c0_, c1_ = b_ * 512, min(NW, (b_ + 1) * 512)
            ctx.dma("pool", W[:, :, c0_:c1_], wv[:, :, c0_:c1_], writes=[wb[b_]])
        h_sb = [ctx.tile([8, TT], F32) for _ in range(2)]
        hb = [Buf(), Buf()]
        sq = ctx.tile([8, TT], F32)
        sqb = Buf()
        rstd = ctx.tile([TT], F32)
        rstdb = Buf()
        xn2 = [ctx.tile([8, TT], BF16) for _ in range(2)]
        xnb2 = [Buf(), Buf()]
        st = [ctx.tile([nfm, TT], BF16) for _ in range(2)]
        stb = [Buf(), Buf()]
        if vofs is not None:
            vst = [ctx.tile([8, 4, 128], BF16) for _ in range(2)]
            vstb = [Buf(), Buf()]
        if kind == "KVF":
            lfp = ctx.tile([S], F32)
            lfb = Buf()
            ftmp = ctx.tile([TT], F32)
            ftb = Buf()
        hview = h_src.rearrange("(c p) t -> p c t", p=128)
        with nc.psum_tensor(self.uname("ps_proj"), [128, 8, 512], F32) as ps:
            ring = PsumRing([ps[:, i, :] for i in range(7)])
            ps_stat, psb_stat = ps[:, 7, :], Buf()
            NTL = S // TT

            def load_h(t_):
                ctx.dma("sp", h_sb[t_ % 2], hview[:, :, t_ * TT:(t_ + 1) * TT], writes=[hb[t_ % 2]])

            def prenorm(t_):
                z_ = t_ % 2
                self.rms_rstd(h_sb[z_], hb[z_], 8, sq, sqb, rstd, rstdb, ps_stat, psb_stat)
                for c in range(8):
                    ctx.op("dve", lambda e: e.scalar_tensor_tensor(out=xn2[z_][:, c, :], in0=h_sb[z_][:, c, :], scalar=self.gain_sb[:, gi, c:c + 1],
                                                                   in1=rstd, op0=ALU.mult, op1=ALU.mult),
                           reads=[hb[z_], rstdb], writes=[xnb2[z_]])
            load_h(0)
            load_h(1)
            prenorm(0)
            for tt in range(NTL):
                s_ = tt % 2
                xn, xnb = xn2[s_], xnb2[s_]
                if tt + 2 < NTL:
                    load_h(tt + 2)
                if tt + 1 < NTL:
                    prenorm(tt + 1)
                for m, (col, sc) in enumerate(fm):
                    pa, pbuf = ring.next()

                    def mm(e, pa=pa, col=col):
                        for c in range(8):
                            ins = e.matmul(pa, W[:, c, col:col + 128], xn[:, c, :], start=(c == 0), stop=(c == 7))
                        return ins
                    ctx.op("pe", mm, reads=[xnb, wb[col // 512]], writes=[pbuf])
                    if m % 2 == 0:
                        ctx.op("act", lambda e: e.activation(out=st[s_][:, m, :], in_=pa, func=AF.Copy, scale=sc), reads=[pbuf], writes=[stb[s_]])
                    else:
                        ctx.op("dve", lambda e: e.tensor_scalar(st[s_][:, m, :], pa, sc, None, op0=ALU.mult), reads=[pbuf], writes=[stb[s_]])
                ctx.dma("pool", fm_dst[0:nfm, :, tt * TT:(tt + 1) * TT].rearrange("m p t -> p m t"), st[s_], reads=[stb[s_]])
                if vofs is not None:
                    for tb in range(4):
                        for half in range(2):
                            pa, pbuf = ring.next()

                            def mm(e, pa=pa, tb=tb, half=half):
                                for c in range(8):
                                    ins = e.matmul(pa, xn[:, c, tb * 128:(tb + 1) * 128],
                                                   W[:, c, vofs + half * 512:vofs + (half + 1) * 512], start=(c == 0), stop=(c == 7))
                                return ins
                            ctx.op("pe", mm, reads=[xnb, wb[(vofs + half * 512) // 512]], writes=[pbuf])
                            src = pa.rearrange("p (g d) -> p g d", g=4)
                            dst = vst[s_][:, half * 4:(half + 1) * 4, tb, :]
                            if (tb + half) % 2 == 0:
                                ctx.op("act", lambda e: e.activation(out=dst, in_=src, func=AF.Copy), reads=[pbuf], writes=[vstb[s_]])
                            else:
                                ctx.op("dve", lambda e: e.tensor_copy(dst, src), reads=[pbuf], writes=[vstb[s_]])
                    ctx.dma("pool", v_dst[:, :, tt * 4:(tt + 1) * 4, :].rearrange("g p b d -> p g b d"), vst[s_], reads=[vstb[s_]])
                if kind == "KVF":
                    pa, pbuf = ring.next()

                    def mm(e, pa=pa):
                        for c in range(8):
                            ins = e.matmul(pa[0:16, :], W[:, c, 2048:2064], xn[:, c, :], start=(c == 0), stop=(c == 7))
                        return ins
                    ctx.op("pe", mm, reads=[xnb, wb[2048 // 512]], writes=[pbuf])
                    ctx.op("act", lambda e: e.activation(out=ftmp[0:16, :], in_=pa[0:16, :], func=AF.Exp, bias=self.negbf[0:16, 0:1], scale=-1.0),
                           reads=[pbuf, self.b_const], writes=[ftb])
                    ctx.op("act", lambda e: e.activation(out=lfp[0:16, tt * TT:(tt + 1) * TT], in_=ftmp[0:16, :], func=AF.Ln, bias=1.0, scale=1.0),
                           reads=[ftb], writes=[lfb])
            if kind == "KVF":
                zer = ctx.tile([S], F32)
                cneg = ctx.tile([S], F32)
                zb, cb_ = Buf(), Buf()
                ctx.op("pool", lambda e: e.memset(zer[0:16, :], 0.0), writes=[zb])
                ctx.op("dve", lambda e: e.tensor_tensor_scan(cneg[0:16, :], lfp[0:16, :], zer[0:16, :], 0.0, op0=ALU.add, op1=ALU.add),
                       reads=[lfb, zb], writes=[cb_])
                pa, pbuf = ring.next()

                def tr(e, pa=pa):
                    for kb in range(32):
                        ins = e.transpose(pa[:, kb * 16:(kb + 1) * 16], cneg[0:16, kb * 128:(kb + 1) * 128], self.identf[0:16, 0:16])
                    return ins
                ctx.op("pe", tr, reads=[cb_, self.b_const], writes=[pbuf])
                ctx.op("dve", lambda e: e.tensor_copy(self.cnegT.rearrange("p a b -> p (a b)"), pa), reads=[pbuf], writes=[self.b_const])
                dblk = ctx.tile([16, 8], F32)
                dbb = Buf()
                crefv = cneg.rearrange("p (q t) -> p q t", t=512)[0:16, :, 0]
                for hh in range(16):
                    ctx.op("dve", lambda e: e.tensor_scalar(dblk[0:16, hh, :], crefv, self.identf[0:16, hh:hh + 1], None, op0=ALU.mult),
                           reads=[cb_, self.b_const], writes=[dbb])
                pa, pbuf = ring.next()
                ctx.op("pe", lambda e: e.matmul(pa[:, 0:128], self.ones_f[0:16, :], dblk[0:16, :, :].rearrange("p a b -> p (a b)"), start=True, stop=True),
                       reads=[dbb, self.b_const], writes=[pbuf])
                ctx.op("dve", lambda e: e.tensor_copy(self.cref.rearrange("p a b -> p (a b)"), pa[:, 0:128]), reads=[pbuf], writes=[self.b_const])
            ctx.barrier()
        ctx.release(m0)

    def phase_att(self, mode, l):
        ctx, nc = self.ctx, self.nc
        m0 = ctx.mark()
        A = mode == "A"
        NU = 8
        q_sb = [ctx.tile([S], BF16) for _ in range(2)]
        k_sb = [ctx.tile([S], BF16) for _ in range(2)]
        v_sb = [ctx.tile([32, 130], BF16) for _ in range(2)]
        qb = [Buf(), Buf()]
        kb_ = [Buf(), Buf()]
        vb = [Buf(), Buf()]
        NP = 4
        p_sb = [ctx.tile([2, 512], BF16) for _ in range(NP)]
        pbf = [Buf() for _ in range(NP)]
        rcf = ctx.tile([16], F32)
        nl = ctx.tile([4], F32)
        o1 = ctx.tile([4, 128], F32)
        o_sb = ctx.tile([4, 128], F32)
        sqo = ctx.tile([4, 128], F32)
        ss = ctx.tile([4], F32)
        rs4 = ctx.tile([4], F32)
        on_bf = ctx.tile([4, 128], BF16)
        oT_st = [ctx.tile([512], BF16) for _ in range(2)]
        ostb = [Buf(), Buf()]
        biasq = ctx.tile([32, 2], F32)
        wq = ctx.tile([32, 2], F32)
        bqb = Buf()
        wqb = Buf()
        NV = 4
        vs_sb = [ctx.tile([2, 65], BF16) for _ in range(NV)]
        vsb = [Buf() for _ in range(NV)]
        Eb = Buf()
        if A:
            E_sb = ctx.tile([8, 2, 128], F32)
            ctx.dma("sp", E_sb.rearrange("p a b c -> p (a b c)"), self.ebig, writes=[Eb])
        acc_sb = [ctx.tile([3, 512], F32) for _ in range(2)]
        acb = [Buf(), Buf()]
        gstep = [0]
        dq = []

        def defer(k, fn):
            dq.append((gstep[0] + k, fn))

        def run_deferred(force=False):
            while dq and (force or dq[0][0] <= gstep[0]):
                dq.pop(0)[1]()
        eb = Buf()
        onb = Buf()
        for s_ in range(2):
            if A:
                ctx.op("dve", lambda e: e.memset(v_sb[s_][:, :, 128:130], 1.0), writes=[vb[s_]])
            else:
                v4 = v_sb[s_].rearrange("p b (h d) -> p b h d", h=2)
                ctx.op("dve", lambda e: e.memset(v4[:, :, :, 64:65], 1.0), writes=[vb[s_]])

        def load_unit(u):
            s_ = u % 2
            if A:
                ctx.dma("sp", q_sb[s_], self.qkT[u], writes=[qb[s_]])
                ctx.dma("sp", k_sb[s_], self.qkT[8 + u], writes=[kb_[s_]])
                ctx.dma("sp", v_sb[s_][:, :, 0:128], self.vA[u], writes=[vb[s_]])
            else:
                ctx.dma("sp", q_sb[s_], self.qkT[u], writes=[qb[s_]])
                ctx.dma("sp", k_sb[s_], self.kTs[u], writes=[kb_[s_]])
                v4 = v_sb[s_].rearrange("p b (h d) -> p b h d", h=2)
                ctx.dma("sp", v4[:, :, :, 0:64], self.vB[u].rearrange("p b (h d) -> p b h d", h=2), writes=[vb[s_]])

        with nc.psum_tensor(self.uname("ps_s"), [128, 2, 2, 512], F32) as ps_s, \
                nc.psum_tensor(self.uname("ps_acc"), [128, 3, 512], F32) as ps_acc, \
                nc.psum_tensor(self.uname("ps_t"), [128, 1024], BF16) as ps_t:
            psb = [Buf(), Buf()]
            accb = Buf()
            ptb = Buf()
            W_ = 129 if A else 65

            def acc(c, j):
                a = c * 4 + j
                per = 3 if A else 7
                bnk, pos = a // per, a % per
                return ps_acc[:, bnk, pos * W_:(pos + 1) * W_]

            load_unit(0)
            for u in range(NU):
                s_ = u % 2
                if u + 1 < NU:
                    load_unit(u + 1)
                steps = [(qt, kb) for qt in range(8) for kb in range(4 * qt + 4)]
                nst = len(steps)

                def emit_S(i):
                    qt, kb = steps[i]
                    jmin = max(0, kb - 4 * qt)
                    c0 = jmin * 128
                    sl = i % 2

                    def mm(e):
                        e.matmul(ps_s[:, sl, 0, c0:512], k_sb[s_][0:64, kb * 128:(kb + 1) * 128], q_sb[s_][0:64, qt * 512 + c0:(qt + 1) * 512],
                                 start=True, stop=True, tile_position=(0, 0))
                        return e.matmul(ps_s[:, sl, 1, c0:512], k_sb[s_][64:128, kb * 128:(kb + 1) * 128], q_sb[s_][64:128, qt * 512 + c0:(qt + 1) * 512],
                                        start=True, stop=True, tile_position=(64, 0))
                    ctx.op("pe", mm, reads=[qb[s_], kb_[s_]], writes=[psb[sl]])
                    pi = i % NP
                    ctx.op("act", lambda e: e.activation(out=p_sb[pi][:, :, c0:512], in_=ps_s[:, sl, :, c0:512], func=AF.Exp),
                           reads=[psb[sl]], writes=[pbf[pi]])
                    j0 = kb - 4 * qt
                    if A and -1 <= j0 <= 3:
                        ja, jb = max(j0, 0), min(j0 + 1, 3)
                        da = ja - j0
                        nbk = jb - ja + 1
                        pv_ = p_sb[pi][:, :, ja * 128:(jb + 1) * 128].rearrange("p c (d q) -> p c d q", d=nbk)
                        ev_ = E_sb[:, u, da:da + nbk, :].unsqueeze(1).to_broadcast([128, 2, nbk, 128])
                        ctx.op("dve", lambda e: e.tensor_tensor(out=pv_, in0=pv_, in1=ev_, op=ALU.mult),
                               reads=[pbf[pi], Eb], writes=[pbf[pi]])
                    if (not A) and 0 <= j0 <= 3:
                        pv_ = p_sb[pi][:, :, j0 * 128:(j0 + 1) * 128]
                        ev_ = self.tri.unsqueeze(1).to_broadcast([128, 2, 128])
                        ctx.op("dve", lambda e: e.tensor_tensor(out=pv_, in0=pv_, in1=ev_, op=ALU.mult),
                               reads=[pbf[pi], self.b_const], writes=[pbf[pi]])

                def emit_VS(i):
                    qt, kb = steps[i]
                    if kb == 0:
                        for hh in range(2):
                            hd = 2 * u + hh
                            ctx.op("dve", lambda e: e.tensor_scalar(biasq[:, :, hh], self.cnegT[:, :, hd], self.cref[:, hd, qt:qt + 1], None, op0=ALU.subtract),
                                   reads=[self.b_const], writes=[bqb])
                        ctx.op("act", lambda e: e.activation(out=wq, in_=biasq, func=AF.Exp), reads=[bqb], writes=[wqb])
                    vi = i % NV
                    v4_ = v_sb[s_].rearrange("p b (h d) -> p b h d", h=2)
                    ctx.op("dve", lambda e: e.tensor_tensor(out=vs_sb[vi], in0=v4_[:, kb, :, :],
                                                            in1=wq[:, kb, :].unsqueeze(2).to_broadcast([128, 2, 65]), op=ALU.mult),
                           reads=[vb[s_], wqb], writes=[vsb[vi]])

                def emit_PV(i):
                    qt, kb = steps[i]
                    jmin = max(0, kb - 4 * qt)
                    pi = i % NP

                    def mm(e):
                        ins = None
                        started = set()
                        per = 3 if A else 7
                        for j in range(jmin, 4):
                            for c in range(2):
                                if A:
                                    rhs = v_sb[s_][:, kb, 0:129]
                                else:
                                    rhs = vs_sb[i % NV][:, c, :]
                                bnk = (c * 4 + j) // per
                                st_ = (kb == 0) and (bnk not in started)
                                started.add(bnk)
                                ins = e.matmul(acc(c, j), p_sb[pi][:, c, j * 128:(j + 1) * 128], rhs,
                                               start=st_, stop=(kb == 4 * qt + j), skip_group_check=True)
                        return ins
                    ctx.op("pe", mm, reads=[pbf[pi], vb[s_]] + ([] if A else [vsb[i % NV]]), writes=[accb])
                    gstep[0] += 1
                    run_deferred()
                    if kb == 4 * qt + 3:
                        emit_epi(qt)

                def emit_epi(qt, u=u):
                    run_deferred(force=True)
                    so = (u * 8 + qt) % 2
                    ea = (u * 8 + qt) % 2
                    nb = 3 if A else 2
                    per_ = 3 if A else 7
                    ncol = per_ * W_
                    ctx.op("dve", lambda e: e.tensor_copy(acc_sb[ea][:, 0:nb, 0:ncol], ps_acc[:, 0:nb, 0:ncol]), reads=[accb], writes=[acb[ea]])

                    def sacc(c, j):
                        a = c * 4 + j
                        per = 3 if A else 7
                        bnk, pos = a // per, a % per
                        return acc_sb[ea][:, bnk, pos * W_:(pos + 1) * W_]
                    sums_ = acc_sb[ea][:, 0:nb, 0:ncol].rearrange("p b (k w) -> p b k w", w=W_)[:, :, :, W_ - 1]
                    ctx.op("dve", lambda e: e.reciprocal(rcf[:, 0:nb * per_].rearrange("p (b k) -> p b k", k=per_), sums_), reads=[acb[ea]], writes=[eb])
                    if A:
                        ctx.op("dve", lambda e: e.tensor_scalar(nl, rcf[:, 4:8], self.neglam[:, l:l + 1], None, op0=ALU.mult), reads=[eb, self.b_const], writes=[eb])
                        a03 = acc_sb[ea][:, 0, 0:3 * W_].rearrange("p (k w) -> p k w", w=W_)[:, :, 0:128]
                        ctx.op("dve", lambda e: e.tensor_tensor(out=o1[:, 0:3, :], in0=a03, in1=rcf[:, 0:3].unsqueeze(2).to_broadcast([128, 3, 128]), op=ALU.mult),
                               reads=[acb[ea], eb], writes=[eb])
                        ctx.op("dve", lambda e: e.tensor_scalar(o1[:, 3, :], sacc(0, 3)[:, 0:128], rcf[:, 3:4], None, op0=ALU.mult),
                               reads=[acb[ea], eb], writes=[eb])
                        for j in range(4):
                            ctx.op("dve", lambda e: e.scalar_tensor_tensor(out=o_sb[:, j, :], in0=sacc(1, j)[:, 0:128], scalar=nl[:, j:j + 1], in1=o1[:, j, :],
                                                                           op0=ALU.mult, op1=ALU.add), reads=[acb[ea], eb], writes=[eb])
                        ctx.op("dve", lambda e: e.tensor_tensor(out=sqo, in0=o_sb, in1=o_sb, op=ALU.mult), reads=[eb], writes=[eb])
                        ctx.op("dve", lambda e: e.reduce_sum(out=ss, in_=sqo, axis=AX.X), reads=[eb], writes=[eb])

                        def e2():
                            ctx.op("act", lambda e: e.activation(out=rs4, in_=ss, func=AF.Ln, bias=SUBLN_EPS, scale=1.0 / 128), reads=[eb], writes=[eb])
                            ctx.op("act", lambda e: e.activation(out=rs4, in_=rs4, func=AF.Exp, scale=-0.5), reads=[eb], writes=[eb])

                        def e3():
                            for j in range(4):
                                ctx.op("dve", lambda e: e.scalar_tensor_tensor(out=on_bf[:, j, :], in0=o_sb[:, j, :], scalar=rs4[:, j:j + 1], in1=self.gsub[:, l, :],
                                                                               op0=ALU.mult, op1=ALU.mult), reads=[eb, self.b_const], writes=[onb])
                        defer(6, e2)
                        defer(9, e3)
                    else:
                        b0 = acc_sb[ea][:, 0, 0:7 * W_].rearrange("p (k w) -> p k w", w=W_)[:, :, 0:64]
                        ctx.op("dve", lambda e: e.tensor_tensor(out=on_bf[:, 0:4, 0:64], in0=b0[:, 0:4, :],
                                                                in1=rcf[:, 0:4].unsqueeze(2).to_broadcast([128, 4, 64]), op=ALU.mult),
                               reads=[acb[ea], eb], writes=[onb])
                        ctx.op("dve", lambda e: e.tensor_tensor(out=on_bf[:, 0:3, 64:128], in0=b0[:, 4:7, :],
                                                                in1=rcf[:, 4:7].unsqueeze(2).to_broadcast([128, 3, 64]), op=ALU.mult),
                               reads=[acb[ea], eb], writes=[onb])
                        ctx.op("dve", lambda e: e.tensor_scalar(on_bf[:, 3, 64:128], sacc(1, 3)[:, 0:64], rcf[:, 7:8], None, op0=ALU.mult),
                               reads=[acb[ea], eb], writes=[onb])

                    def e4():
                        def tr(e):
                            for j in range(4):
                                ins = e.transpose(ps_t[:, j * 128:(j + 1) * 128], on_bf[:, j, :], self.ident_bf)
                            return ins
                        ctx.op("pe", tr, reads=[onb, self.b_const], writes=[ptb])
                        ctx.op("dve", lambda e: e.tensor_copy(oT_st[so], ps_t[:, 0:512]), reads=[ptb], writes=[ostb[so]])
                        ctx.dma("pool", self.oT[u * 128:(u + 1) * 128, qt * 512:(qt + 1) * 512], oT_st[so], reads=[ostb[so]])
                    defer(12 if A else 6, e4)

                emit_S(0)
                if not A:
                    for v_ in range(min(3, nst)):
                        emit_VS(v_)
                for i in range(nst):
                    if i + 1 < nst:
                        emit_S(i + 1)
                    if (not A) and i + 3 < nst:
                        emit_VS(i + 3)
                    emit_PV(i)
            run_deferred(force=True)
            ctx.barrier()
        ctx.release(m0)

    def phase_po(self, l, h_src, w_o, gi):
        ctx, nc = self.ctx, self.nc
        m0 = ctx.mark()
        TT = 512
        W = ctx.tile([8, D], BF16)
        wb = [Buf() for _ in range(8)]
        for c in range(8):
            ctx.dma("pool", W[:, c, :], w_o[c * 128:(c + 1) * 128, :], writes=[wb[c]])
        o_sb = [ctx.tile([8, TT], BF16) for _ in range(2)]
        ob = [Buf(), Buf()]
        h_sb = [ctx.tile([8, TT], F32) for _ in range(2)]
        hb = [Buf(), Buf()]
        a_sb2 = [ctx.tile([8, TT], F32) for _ in range(2)]
        ab2 = [Buf(), Buf()]
        sq = ctx.tile([8, TT], F32)
        sqb = Buf()
        rstd = ctx.tile([TT], F32)
        rstdb = Buf()
        hview = h_src.rearrange("(c p) t -> p c t", p=128)
        oview = self.oT.rearrange("(c p) t -> p c t", p=128)
        dview = self.hbuf.rearrange("(c p) t -> p c t", p=128)
        with nc.psum_tensor(self.uname("ps_po"), [128, 8, 512], F32) as ps:
            ring = PsumRing([ps[:, i, :] for i in range(7)])
            ps_stat, psb_stat = ps[:, 7, :], Buf()
            NTL = S // TT

            def load(tt):
                s_ = tt % 2
                ctx.dma("sp", o_sb[s_], oview[:, :, tt * TT:(tt + 1) * TT], writes=[ob[s_]])
                ctx.dma("sp", h_sb[s_], hview[:, :, tt * TT:(tt + 1) * TT], writes=[hb[s_]])

            def mmpart(tt, m_lo, m_hi):
                s_ = tt % 2
                a_sb, ab = a_sb2[s_], ab2[s_]
                evs = []
                for m in range(m_lo, m_hi):
                    pa, pbuf = ring.next()

                    def mm(e, pa=pa, m=m):
                        for c in range(8):
                            ins = e.matmul(pa, W[:, c, m * 128:(m + 1) * 128], o_sb[s_][:, c, :], start=(c == 0), stop=(c == 7))
                        return ins
                    ctx.op("pe", mm, reads=[ob[s_]] + wb, writes=[pbuf])

                    def ev(pa=pa, pbuf=pbuf, m=m):
                        if m % 2 == 0:
                            ctx.op("act", lambda e: e.activation(out=a_sb[:, m, :], in_=pa, func=AF.Copy), reads=[pbuf], writes=[ab])
                        else:
                            ctx.op("dve", lambda e: e.tensor_copy(a_sb[:, m, :], pa), reads=[pbuf], writes=[ab])
                    evs.append(ev)
                return evs

            def post1(tt):
                s_ = tt % 2
                self.rms_rstd(a_sb2[s_], ab2[s_], 8, sq, sqb, rstd, rstdb, ps_stat, psb_stat)

            def post2(tt):
                s_ = tt % 2
                a_sb, ab = a_sb2[s_], ab2[s_]
                for c in range(8):
                    ctx.op("dve", lambda e: e.scalar_tensor_tensor(out=a_sb[:, c, :], in0=a_sb[:, c, :], scalar=self.gain_sb[:, gi, c:c + 1], in1=rstd,
                                                                   op0=ALU.mult, op1=ALU.mult), reads=[ab, rstdb], writes=[ab])
                ctx.op("dve", lambda e: e.tensor_tensor(out=h_sb[s_], in0=h_sb[s_], in1=a_sb, op=ALU.add), reads=[ab, hb[s_]], writes=[hb[s_]])
                ctx.dma("pool", dview[:, :, tt * TT:(tt + 1) * TT], h_sb[s_], reads=[hb[s_]])

            load(0)
            for ev in mmpart(0, 0, 4):
                ev()
            for ev in mmpart(0, 4, 8):
                ev()
            for tt in range(NTL):
                if tt + 1 < NTL:
                    load(tt + 1)
                    evs = mmpart(tt + 1, 0, 4)
                    post1(tt)
                    for ev in evs:
                        ev()
                    for ev in mmpart(tt + 1, 4, 8):
                        ev()
                    post2(tt)
                else:
                    post1(tt)
                    post2(tt)
            ctx.barrier()
        ctx.release(m0)

    def phase_ffn(self, l, h_dst):
        ctx, nc = self.ctx, self.nc
        m0 = ctx.mark()
        TT = 256
        NTL = S // TT
        KPRE = 3
        Wi = ctx.tile([8, 2 * F], BF16)
        wib = {}
        wiv = self.ffn_w_in[l].rearrange("(c p) n -> p c n", p=128)
        for blk in range(NFC // 2):
            for gv in range(2):
                c0_ = gv * F + blk * 256
                wib[(gv, blk)] = Buf()
                ctx.dma("pool", Wi[:, :, c0_:c0_ + 256], wiv[:, :, c0_:c0_ + 256], writes=[wib[(gv, blk)]])
        Wo = ctx.tile([NFC, D], BF16)
        wob = Buf()
        wov = self.ffn_w_out[l].rearrange("(c p) n -> p c n", p=128)
        for c0 in range(0, NFC, 6):
            c1 = min(NFC, c0 + 6)
            ctx.dma("pool", Wo[:, c0:c1, :], wov[:, c0:c1, :], writes=[wob])
        cw = ctx.tile([4, 44], F32)
        cwb = Buf()
        ctx.dma("sp", cw.rearrange("p a b -> p (a b)"), self.convp[:, l * 176:(l + 1) * 176], writes=[cwb])
        h_sb = [ctx.tile([8, TT], F32) for _ in range(2)]
        hb = [Buf(), Buf()]
        sq = ctx.tile([8, TT], F32)
        sqb = Buf()
        rstd = ctx.tile([TT], F32)
        rstdb = Buf()
        xn = [ctx.tile([8, TT], BF16) for _ in range(2)]
        xnb = [Buf(), Buf()]
        NU_, NY_ = 3, 4
        u_sb = [ctx.tile([2, TT + 2], F32) for _ in range(NU_)]
        ub = [Buf() for _ in range(NU_)]
        uhb = [Buf() for _ in range(NU_)]
        y_sb = [ctx.tile([2, TT], F32) for _ in range(NY_)]
        yb = [Buf() for _ in range(NY_)]
        halo = ctx.tile([NFC, 2, 2], F32)
        halob = [Buf() for _ in range(NFC)]
        g_sb = ctx.tile([NFC, TT], BF16)
        gb = [Buf() for _ in range(NFC)]
        f_sb = ctx.tile([8, TT], F32)
        fb = Buf()
        rstd2 = ctx.tile([TT], F32)
        rstd2b = Buf()
        hview = self.hbuf.rearrange("(c p) t -> p c t", p=128)
        dview = h_dst.rearrange("(c p) t -> p c t", p=128)
        ctx.op("pool", lambda e: e.memset(halo, 0.0), writes=halob)
        with nc.psum_tensor(self.uname("ps_ffn"), [128, 8, 512], F32) as ps:
            uring = PsumRing([ps[:, i, :].rearrange("p (a t) -> p a t", a=2) for i in range(5)])
            fps = [ps[:, 5 + i, :].rearrange("p (a t) -> p a t", a=2) for i in range(2)]
            fpb = [Buf() for _ in range(2)]
            ps_stat, psb_stat = ps[:, 7, 0:TT], Buf()
            cnt = [0]

            def load(tt):
                s_ = tt % 2
                ctx.dma("sp", h_sb[s_], hview[:, :, tt * TT:(tt + 1) * TT], writes=[hb[s_]])

            def prenorm(tt):
                s_ = tt % 2
                self.rms_rstd(h_sb[s_], hb[s_], 8, sq, sqb, rstd, rstdb, ps_stat, psb_stat)
                for c in range(8):
                    ctx.op("dve", lambda e: e.scalar_tensor_tensor(out=xn[s_][:, c, :], in0=h_sb[s_][:, c, :], scalar=self.gain_sb[:, G_F_PRE + l, c:c + 1],
                                                                   in1=rstd, op0=ALU.mult, op1=ALU.mult),
                           reads=[hb[s_], rstdb], writes=[xnb[s_]])

            def pair(tt, i):
                s_ = tt % 2
                n = cnt[0]
                cnt[0] += 1
                us, ys = n % NU_, n % NY_
                pa, pbuf = uring.next()

                def mm(e):
                    for gv, col in ((0, i * 128), (1, F + i * 128)):
                        for c in range(8):
                            ins = e.matmul(pa[:, gv, :], Wi[:, c, col:col + 128], xn[s_][:, c, :], start=(c == 0), stop=(c == 7))
                    return ins
                ctx.op("pe", mm, reads=[xnb[s_], wib[(0, i // 2)], wib[(1, i // 2)]], writes=[pbuf])
                chs = (i, i + NFC)
                ctx.op("pool", lambda e: e.tensor_copy(u_sb[us][:, :, 0:2], halo[:, i, :, :]), reads=[halob[i]], writes=[uhb[us]])
                ctx.op("act", lambda e: e.activation(out=u_sb[us][:, :, 2:TT + 2], in_=pa, func=AF.Copy), reads=[pbuf], writes=[ub[us]])
                for gv in range(2):
                    ch = chs[gv]
                    ctx.op("act", lambda e: e.activation(out=y_sb[ys][:, gv, :], in_=pa[:, gv, :], func=AF.Identity,
                                                         scale=cw[:, 2, ch:ch + 1], bias=cw[:, 3, ch:ch + 1]),
                           reads=[pbuf, cwb], writes=[yb[ys]])
                ctx.op("pool", lambda e: e.tensor_copy(halo[:, i, :, :], u_sb[us][:, :, TT:TT + 2]), reads=[ub[us]], writes=[halob[i]])
                for tap, off in ((1, 1), (0, 0)):
                    for gv in range(2):
                        ch = chs[gv]
                        ctx.op("dve", lambda e: e.scalar_tensor_tensor(out=y_sb[ys][:, gv, :], in0=u_sb[us][:, gv, off:off + TT], scalar=cw[:, tap, ch:ch + 1],
                                                                       in1=y_sb[ys][:, gv, :], op0=ALU.mult, op1=ALU.add),
                               reads=[ub[us], uhb[us], yb[ys], cwb], writes=[yb[ys]])
                def stage_c():
                    ctx.op("act", lambda e: e.activation(out=y_sb[ys][:, 0, :], in_=y_sb[ys][:, 0, :], func=AF.Gelu_apprx_tanh), reads=[yb[ys]], writes=[yb[ys]])

                def stage_d():
                    ctx.op("pool", lambda e: e.tensor_tensor(out=g_sb[:, i, :], in0=y_sb[ys][:, 0, :], in1=y_sb[ys][:, 1, :], op=ALU.mult),
                           reads=[yb[ys]], writes=[gb[i]])
                return stage_c, stage_d

            def wout_part1(tt):
                for m in range(8):
                    k_ = (m // 2) % 2
                    pa, pbuf = fps[k_][:, m % 2, :], fpb[k_]

                    def mm(e, pa=pa, m=m):
                        for c in range(NFC):
                            ins = e.matmul(pa, Wo[:, c, m * 128:(m + 1) * 128], g_sb[:, c, :], start=(c == 0), stop=(c == NFC - 1))
                        return ins
                    ctx.op("pe", mm, reads=gb + [wob], writes=[pbuf])
                    if m % 2 == 1:
                        src = fps[k_]
                        dst = f_sb[:, m - 1:m + 1, :]
                        if (m // 2) % 2 == 0:
                            ctx.op("dve", lambda e: e.tensor_copy(dst, src), reads=[pbuf], writes=[fb])
                        else:
                            ctx.op("act", lambda e: e.activation(out=dst, in_=src, func=AF.Copy), reads=[pbuf], writes=[fb])
                ctx.op("act", lambda e: e.activation(out=sq, in_=f_sb, func=AF.Square), reads=[fb], writes=[sqb])
                n_ = 8
                while n_ > 1:
                    hl = n_ // 2
                    ctx.op("dve", lambda e: e.tensor_tensor(out=sq[:, 0:hl, :], in0=sq[:, 0:hl, :], in1=sq[:, hl:2 * hl, :], op=ALU.add),
                           reads=[sqb], writes=[sqb])
                    n_ = hl

            def wout_part2(tt):
                s_ = tt % 2
                ctx.op("pe", lambda e: e.matmul(ps_stat, self.ones_mean, sq[:, 0, :], start=True, stop=True), reads=[sqb, self.b_const], writes=[psb_stat])
                ctx.op("act", lambda e: e.activation(out=rstd2, in_=ps_stat, func=AF.Ln, bias=RMS_EPS, scale=1.0), reads=[psb_stat], writes=[rstd2b])
                ctx.op("act", lambda e: e.activation(out=rstd2, in_=rstd2, func=AF.Exp, scale=-0.5), reads=[rstd2b], writes=[rstd2b])
                for c in range(8):
                    ctx.op("dve", lambda e: e.scalar_tensor_tensor(out=f_sb[:, c, :], in0=f_sb[:, c, :], scalar=self.gain_sb[:, G_F_POST + l, c:c + 1], in1=rstd2,
                                                                   op0=ALU.mult, op1=ALU.mult), reads=[fb, rstd2b], writes=[fb])
                ctx.op("dve", lambda e: e.tensor_tensor(out=h_sb[s_], in0=h_sb[s_], in1=f_sb, op=ALU.add), reads=[fb, hb[s_]], writes=[hb[s_]])
                ctx.dma("sp", dview[:, :, tt * TT:(tt + 1) * TT], h_sb[s_], reads=[hb[s_]])

            LAG = 2
            pend = []

            def push(cd):
                pend.append(cd)
                while len(pend) > LAG:
                    c_, d_ = pend.pop(0)
                    c_()
                    d_()

            def flush(do_d=True):
                ds = []
                while pend:
                    c_, d_ = pend.pop(0)
                    c_()
                    if do_d:
                        d_()
                    else:
                        ds.append(d_)
                return ds

            load(0)
            prenorm(0)
            for i in range(KPRE):
                push(pair(0, i))
            for tt in range(NTL):
                for i in range(KPRE, NFC):
                    push(pair(tt, i))
                    if i == 6 and tt > 0:
                        wout_part2(tt - 1)
                    if i == 8 and tt + 1 < NTL:
                        load(tt + 1)
                    if i == 14 and tt + 1 < NTL:
                        prenorm(tt + 1)
                flush()
                ds = []
                if tt + 1 < NTL:
                    for i in range(KPRE):
                        pend.append(pair(tt + 1, i))
                    ds = flush(do_d=False)
                wout_part1(tt)
                for d_ in ds:
                    d_()
            wout_part2(NTL - 1)
            ctx.barrier()
        ctx.release(m0)


def _pack_small(inp):
    def pc(v):
        return np.ascontiguousarray(np.asarray(v, np.float32).reshape(8, 128).T)
    gl = []
    for name, n in (("a_norm_pre", 2), ("a_norm_post", 2)):
        gl += [pc(inp[name][i]) for i in range(n)]
    gl.append(pc(inp["kv_norm"]))
    for name, n in (("b_norm_pre", 2), ("b_norm_post", 2), ("ffn_norm_pre", 4), ("ffn_norm_post", 4)):
        gl += [pc(inp[name][i]) for i in range(n)]
    gains = np.ascontiguousarray(np.stack(gl, axis=1).reshape(128, NG * 8))
    cw = np.asarray(inp["ffn_conv_w"], np.float32)
    cb = np.asarray(inp["ffn_conv_b"], np.float32)
    allp = np.concatenate([cw, cb[:, None, :]], axis=1)
    convp = np.ascontiguousarray(allp.reshape(4, 4, 44, 128).transpose(3, 0, 1, 2).reshape(128, 4 * 4 * 44))
    lamv = np.stack([np.stack([inp["a_lam_q1"][l], inp["a_lam_k1"][l], inp["a_lam_q2"][l], inp["a_lam_k2"][l]]) for l in range(2)])
    lamv = np.ascontiguousarray(np.asarray(lamv, np.float32).reshape(1, 512))
    subln = np.ascontiguousarray(np.asarray(inp["a_subln"], np.float32).reshape(1, 256))
    b_f = np.ascontiguousarray(np.asarray(inp["b_f"], np.float32).reshape(16, 1))
    return gains, convp, lamv, subln, b_f


_PROG_CACHE = {}


def _get_prog(key, **kw):
    if key not in _PROG_CACHE:
        _PROG_CACHE[key] = Prog(**kw)
    return _PROG_CACHE[key]


def _in_maps(inp, xTs):
    gains, convp, lamv, subln, b_f = _pack_small(inp)
    cst = _make_consts()
    shared = {
        "rel_bias": np.ascontiguousarray(inp["rel_bias"], np.float32),
        "a_w_qkv": np.ascontiguousarray(inp["a_w_qkv"], np.float32),
        "a_w_o": np.ascontiguousarray(inp["a_w_o"], np.float32),
        "w_kvf": np.ascontiguousarray(inp["w_kvf"], np.float32),
        "b_w_q": np.ascontiguousarray(inp["b_w_q"], np.float32),
        "b_w_o": np.ascontiguousarray(inp["b_w_o"], np.float32),
        "ffn_w_in": np.ascontiguousarray(inp["ffn_w_in"], np.float32),
        "ffn_w_out": np.ascontiguousarray(inp["ffn_w_out"], np.float32),
        "gains": gains, "convp": convp, "lamv": lamv, "subln": subln, "b_f": b_f, "cst": cst,
    }
    return [dict(shared, xT=xTs[b]) for b in range(len(xTs))]


def kernel(**inp):
    x = np.asarray(inp["x"], np.float32)
    B = x.shape[0]
    xTs = [np.ascontiguousarray(x[b].T) for b in range(B)]
    prog = _get_prog("full")
    res = run_bass_kernel_spmd(prog.nc, _in_maps(inp, xTs), core_ids=list(range(B)))
    out = np.empty((B, S, D), np.float32)
    for b in range(B):
        out[b] = res.results[b]["yT"].T
    return out
```
